# Optimizing a Trainium2 kernel written in Bass

```python
import math
import jax, jax.numpy as jnp
from jax import lax
import numpy as np

D_MODEL = 1024
BATCH = 8
SEQ = 4096
DEPTH = 2

GRID_W = 64
CTX_LEN = 256
ATT_HEADS = 8
ATT_DH = 64
ATT_DV = 2 * ATT_DH
ATT_QBLOCK = 128
GM_WIDTH = 1024
GM_GROUPS = 8
GM_CHUNK = 128
HG_HEADS = 8
HG_DK = 128
HG_DV = 128
HG_CHUNK = 64
N_BRANCH = 3
D_FF = 2816
CONV_W = 3
ROPE_BASE = 10000.0
EPS = 1e-6

ATT_QW = ATT_HEADS * 2 * ATT_DH
ATT_VW = ATT_HEADS * ATT_DV
HG_KW = HG_HEADS * HG_DK
HG_VW = HG_HEADS * HG_DV
IN_SPLITS = (ATT_QW, ATT_QW, ATT_VW, GM_WIDTH, GM_WIDTH, HG_KW, HG_KW, HG_KW, HG_VW, HG_VW, N_BRANCH * D_MODEL)
IN_W = sum(IN_SPLITS)

kernel_name = "hybrid_diffattn_gmlp_hgrn2_dit_block"


def rmsnorm(x, g):
    xf = x.astype(jnp.float32)
    y = xf * lax.rsqrt(jnp.mean(xf * xf, axis=-1, keepdims=True) + EPS)
    return (y * g.astype(jnp.float32)).astype(x.dtype)


def layernorm(x, g, b):
    xf = x.astype(jnp.float32)
    mu = jnp.mean(xf, axis=-1, keepdims=True)
    var = jnp.mean(jnp.square(xf - mu), axis=-1, keepdims=True)
    y = (xf - mu) * lax.rsqrt(var + EPS)
    return (y * g.astype(jnp.float32) + b.astype(jnp.float32)).astype(x.dtype)


def modulate(x, shift, scale):
    return x * (1 + scale) + shift


def split_cols(p):
    out, start = [], 0
    for w in IN_SPLITS:
        out.append(p[..., start:start + w])
        start += w
    return out


def rope_1d(x, pos):
    half = x.shape[-1] // 2
    inv = ROPE_BASE ** (-jnp.arange(half, dtype=jnp.float32) / half)
    ang = pos.astype(jnp.float32)[:, None] * inv[None, :]
    cos = jnp.cos(ang)[None, :, None, None, :].astype(x.dtype)
    sin = jnp.sin(ang)[None, :, None, None, :].astype(x.dtype)
    x1, x2 = x[..., :half], x[..., half:]
    return jnp.concatenate([x1 * cos - x2 * sin, x2 * cos + x1 * sin], axis=-1)


def axial_rope(x, rows, cols):
    r = ATT_DH // 2
    return jnp.concatenate([rope_1d(x[..., :r], rows), rope_1d(x[..., r:], cols)], axis=-1)


def diff_attention(q, k, v, lam):
    B, T = q.shape[0], q.shape[1]
    nb = T // ATT_QBLOCK
    qb = q.reshape(B, nb, ATT_QBLOCK, ATT_HEADS, 2, ATT_DH).transpose(1, 0, 2, 3, 4, 5)
    scale = ATT_DH ** -0.5

    def block(qi):
        s = jnp.einsum('bqhid,bkhid->bhiqk', qi, k).astype(jnp.float32) * scale
        p = jax.nn.softmax(s, axis=-1)
        w = (p[:, :, 0] - lam * p[:, :, 1]).astype(v.dtype)
        return jnp.einsum('bhqk,bkhv->bqhv', w, v)

    o = lax.map(block, qb)
    return o.transpose(1, 0, 2, 3, 4).reshape(B, T, ATT_HEADS, ATT_DV)


def spatial_gating(u, v, ln_g, ln_b, w_s, b_s):
    B, T, _ = v.shape
    n = T // GM_CHUNK
    dg = GM_WIDTH // GM_GROUPS
    vn = layernorm(v, ln_g, ln_b).reshape(B, n, GM_CHUNK, GM_GROUPS, dg)
    s = jnp.einsum('gts,bnsgc->bntgc', w_s, vn) + b_s.T[None, None, :, :, None]
    return u * s.reshape(B, T, GM_WIDTH)


def hgrn2_gates(z, lb):
    B, T, _ = z.shape
    zf = z.astype(jnp.float32)
    logf = jnp.logaddexp(jnp.log(lb), jnp.log1p(-lb) + jax.nn.log_sigmoid(zf))
    k = ((1 - lb) * jax.nn.sigmoid(-zf)).astype(z.dtype)
    return logf.reshape(B, T, HG_HEADS, HG_DK), k.reshape(B, T, HG_HEADS, HG_DK)


def hgrn2_scan(k, v, logf, s0, q=None):
    B, T = k.shape[0], k.shape[1]
    nc = T // HG_CHUNK
    with_output = q is not None

    def to_chunks(a):
        return a.reshape(B, nc, HG_CHUNK, a.shape[2], a.shape[3]).transpose(1, 0, 3, 2, 4)

    tri = jnp.tril(jnp.ones((HG_CHUNK, HG_CHUNK), dtype=bool))
    xs = (to_chunks(k), to_chunks(v), to_chunks(logf)) + ((to_chunks(q),) if with_output else ())

    def step(S, inp):
        k_, v_, lf = inp[0], inp[1], inp[2]
        A = jnp.cumsum(lf, axis=2)
        A_last = A[:, :, -1:, :]
        S_new = jnp.exp(A_last[:, :, 0, :])[..., None] * S + jnp.einsum(
            'bhsk,bhsv->bhkv', k_ * jnp.exp(A_last - A), v_)
        if not with_output:
            return S_new, None
        q_ = inp[3]
        o_inter = jnp.einsum('bhtk,bhkv->bhtv', q_ * jnp.exp(A), S)
        diff = A[:, :, :, None, :] - A[:, :, None, :, :]
        decay = jnp.exp(jnp.where(tri[None, None, :, :, None], diff, -jnp.inf))
        scores = jnp.einsum('bhtk,bhsk,bhtsk->bhts', q_, k_, decay)
        return S_new, o_inter + jnp.einsum('bhts,bhsv->bhtv', scores, v_)

    S, o = lax.scan(step, s0, xs)
    if with_output:
        o = o.transpose(1, 0, 3, 2, 4).reshape(B, T, HG_HEADS, HG_DV)
    return S, o


def flip(a):
    return jnp.flip(a, axis=1)


def branch_merge(y_att, y_gm, y_hg, gates, w_br_att, w_br_gm, w_br_hg, w_out):
    g_att, g_gm, g_hg = jnp.split(jax.nn.sigmoid(gates), N_BRANCH, axis=-1)
    y = g_att * (y_att @ w_br_att) + g_gm * (y_gm @ w_br_gm) + g_hg * (y_hg @ w_br_hg)
    return y @ w_out


def token_mixers(h, hc, rows, cols, lam_init, lb, with_ctx_out, w_in, lam_q1, lam_k1, lam_q2, lam_k2,
                 att_subln_g, gm_ln_g, gm_ln_b, gm_ws, gm_bs, hg_norm_g, w_br_att, w_br_gm, w_br_hg, w_out):
    B, T, _ = h.shape
    L = hc.shape[1]
    aq, ak, av, gu, gv, hq, hff, hfb, hi, hg, gates = split_cols(h @ w_in)
    caq, cak, cav, cgu, cgv, chq, chff, chfb, chi, chg, cgates = split_cols(hc @ w_in)

    lam = (jnp.exp(jnp.sum(lam_q1.astype(jnp.float32) * lam_k1.astype(jnp.float32)))
           - jnp.exp(jnp.sum(lam_q2.astype(jnp.float32) * lam_k2.astype(jnp.float32))) + lam_init)
    q = axial_rope(aq.reshape(B, T, ATT_HEADS, 2, ATT_DH), rows, cols)
    k = axial_rope(ak.reshape(B, T, ATT_HEADS, 2, ATT_DH), rows, cols)
    kc = cak.reshape(B, L, ATT_HEADS, 2, ATT_DH)
    vc = cav.reshape(B, L, ATT_HEADS, ATT_DV)
    keys = jnp.concatenate([k, kc], axis=1)
    vals = jnp.concatenate([av.reshape(B, T, ATT_HEADS, ATT_DV), vc], axis=1)

    def att_out(o):
        return (rmsnorm(o, att_subln_g) * (1 - lam_init)).reshape(o.shape[0], o.shape[1], ATT_VW)

    y_att = att_out(diff_attention(q, keys, vals, lam))

    y_gm = spatial_gating(jax.nn.gelu(gu), jax.nn.gelu(gv), gm_ln_g, gm_ln_b, gm_ws, gm_bs)

    lf_f, k_f = hgrn2_gates(hff, lb[0])
    lf_b, k_b = hgrn2_gates(hfb, lb[1])
    clf_f, ck_f = hgrn2_gates(chff, lb[0])
    clf_b, ck_b = hgrn2_gates(chfb, lb[1])
    qh = jax.nn.silu(hq).reshape(B, T, HG_HEADS, HG_DK)
    iv = hi.reshape(B, T, HG_HEADS, HG_DV)
    civ = chi.reshape(B, L, HG_HEADS, HG_DV)
    s0 = jnp.zeros((B, HG_HEADS, HG_DK, HG_DV), jnp.float32)
    cq = jax.nn.silu(chq).reshape(B, L, HG_HEADS, HG_DK) if with_ctx_out else None
    s_f, oc_f = hgrn2_scan(ck_f, civ, clf_f, s0, q=cq)
    s_b, oc_b = hgrn2_scan(flip(ck_b), flip(civ), flip(clf_b), s0, q=None if cq is None else flip(cq))
    _, o_f = hgrn2_scan(k_f, iv, lf_f, s_f, q=qh)
    _, o_b = hgrn2_scan(flip(k_b), flip(iv), flip(lf_b), s_b, q=flip(qh))

    def hg_out(o, g):
        Bo, To = o.shape[0], o.shape[1]
        o = rmsnorm(o.astype(h.dtype), hg_norm_g) * jax.nn.silu(g.reshape(Bo, To, HG_HEADS, HG_DV))
        return o.reshape(Bo, To, HG_VW)

    y_hg = hg_out(o_f + flip(o_b), hg)
    y = branch_merge(y_att, y_gm, y_hg, gates, w_br_att, w_br_gm, w_br_hg, w_out)

    if not with_ctx_out:
        return y, None
    yc_att = att_out(diff_attention(cak.reshape(B, L, ATT_HEADS, 2, ATT_DH) * 0 + caq.reshape(B, L, ATT_HEADS, 2, ATT_DH), kc, vc, lam))
    yc_gm = spatial_gating(jax.nn.gelu(cgu), jax.nn.gelu(cgv), gm_ln_g, gm_ln_b, gm_ws, gm_bs)
    yc_hg = hg_out(oc_f + flip(oc_b), chg)
    yc = branch_merge(yc_att, yc_gm, yc_hg, cgates, w_br_att, w_br_gm, w_br_hg, w_out)
    return y, yc


def conv_ffn(h, w_up, conv_w, conv_b, w_down):
    T = h.shape[1]
    u = h @ w_up
    pad = CONV_W // 2
    up = jnp.pad(u, ((0, 0), (pad, pad), (0, 0)))
    u = sum(up[:, j:j + T] * conv_w[j] for j in range(CONV_W)) + conv_b
    a, b = jnp.split(u, 2, axis=-1)
    return (jax.nn.silu(a) * b) @ w_down


def setup_inputs(seed: int = 0) -> dict:
    key = jax.random.key(seed)
    ks = jax.random.split(key, 32)
    D = D_MODEL

    def nrm(k, shape, s):
        return jax.random.normal(k, shape, jnp.float32) * s

    return {
        "x": nrm(ks[0], (BATCH, SEQ, D), 1.0),
        "c": nrm(ks[1], (BATCH, D), 1.0),
        "ctx": nrm(ks[2], (BATCH, CTX_LEN, D), 1.0),
        "c_ctx": nrm(ks[3], (D,), 1.0),
        "w_ada": nrm(ks[4], (DEPTH, D, 6 * D), 0.5 * D ** -0.5),
        "b_ada": nrm(ks[5], (DEPTH, 6 * D), 0.01),
        "g_pre_mix": 1.0 + nrm(ks[6], (DEPTH, D), 0.02),
        "g_post_mix": 1.0 + nrm(ks[7], (DEPTH, D), 0.02),
        "g_pre_ffn": 1.0 + nrm(ks[8], (DEPTH, D), 0.02),
        "g_post_ffn": 1.0 + nrm(ks[9], (DEPTH, D), 0.02),
        "w_in": nrm(ks[10], (DEPTH, D, IN_W), D ** -0.5),
        "lam_q1": nrm(ks[11], (DEPTH, ATT_DH), 0.1),
        "lam_k1": nrm(ks[12], (DEPTH, ATT_DH), 0.1),
        "lam_q2": nrm(ks[13], (DEPTH, ATT_DH), 0.1),
        "lam_k2": nrm(ks[14], (DEPTH, ATT_DH), 0.1),
        "att_subln_g": 1.0 + nrm(ks[15], (DEPTH, ATT_DV), 0.02),
        "gm_ln_g": 1.0 + nrm(ks[16], (DEPTH, GM_WIDTH), 0.02),
        "gm_ln_b": nrm(ks[17], (DEPTH, GM_WIDTH), 0.01),
        "gm_ws": nrm(ks[18], (DEPTH, GM_GROUPS, GM_CHUNK, GM_CHUNK), GM_CHUNK ** -0.5),
        "gm_bs": 1.0 + nrm(ks[19], (DEPTH, GM_GROUPS, GM_CHUNK), 0.01),
        "hg_lb": nrm(ks[20], (DEPTH, 2, HG_KW), 0.5),
        "hg_norm_g": 1.0 + nrm(ks[21], (DEPTH, HG_DV), 0.02),
        "w_br_att": nrm(ks[22], (DEPTH, ATT_VW, D), ATT_VW ** -0.5),
        "w_br_gm": nrm(ks[23], (DEPTH, GM_WIDTH, D), GM_WIDTH ** -0.5),
        "w_br_hg": nrm(ks[24], (DEPTH, HG_VW, D), HG_VW ** -0.5),
        "w_out": nrm(ks[25], (DEPTH, D, D), D ** -0.5),
        "w_up": nrm(ks[26], (DEPTH, D, 2 * D_FF), D ** -0.5),
        "conv_w": nrm(ks[27], (DEPTH, CONV_W, 2 * D_FF), 0.5),
        "conv_b": nrm(ks[28], (DEPTH, 2 * D_FF), 0.01),
        "w_down": nrm(ks[29], (DEPTH, D_FF, D), D_FF ** -0.5),
    }


def reference(x, c, ctx, c_ctx, w_ada, b_ada, g_pre_mix, g_post_mix, g_pre_ffn, g_post_ffn, w_in,
              lam_q1, lam_k1, lam_q2, lam_k2, att_subln_g, gm_ln_g, gm_ln_b, gm_ws, gm_bs, hg_lb,
              hg_norm_g, w_br_att, w_br_gm, w_br_hg, w_out, w_up, conv_w, conv_b, w_down):
    T = x.shape[1]
    ROWS = T // GRID_W
    rows = jnp.repeat(jnp.arange(ROWS, dtype=jnp.int32), GRID_W)
    cols = jnp.tile(jnp.arange(GRID_W, dtype=jnp.int32), ROWS)
    lb_all = jnp.cumsum(jax.nn.softmax(hg_lb.astype(jnp.float32), axis=0), axis=0)
    lb_all = lb_all - lb_all[0]
    xc = ctx
    for l in range(DEPTH):
        last = l == DEPTH - 1
        lam_init = 0.8 - 0.6 * math.exp(-0.3 * l)
        mod = (jax.nn.silu(c) @ w_ada[l] + b_ada[l])[:, None, :]
        modc = (jax.nn.silu(c_ctx) @ w_ada[l] + b_ada[l])[None, None, :]
        sh1, sc1, gt1, sh2, sc2, gt2 = jnp.split(mod, 6, axis=-1)
        csh1, csc1, cgt1, csh2, csc2, cgt2 = jnp.split(modc, 6, axis=-1)

        h = modulate(rmsnorm(x, g_pre_mix[l]), sh1, sc1)
        hc = modulate(rmsnorm(xc, g_pre_mix[l]), csh1, csc1)
        y, yc = token_mixers(h, hc, rows, cols, lam_init, lb_all[l], not last, w_in[l],
                             lam_q1[l], lam_k1[l], lam_q2[l], lam_k2[l], att_subln_g[l],
                             gm_ln_g[l], gm_ln_b[l], gm_ws[l], gm_bs[l], hg_norm_g[l],
                             w_br_att[l], w_br_gm[l], w_br_hg[l], w_out[l])
        x = x + gt1 * rmsnorm(y, g_post_mix[l])
        h = modulate(rmsnorm(x, g_pre_ffn[l]), sh2, sc2)
        x = x + gt2 * rmsnorm(conv_ffn(h, w_up[l], conv_w[l], conv_b[l], w_down[l]), g_post_ffn[l])
        if not last:
            xc = xc + cgt1 * rmsnorm(yc, g_post_mix[l])
            hc = modulate(rmsnorm(xc, g_pre_ffn[l]), csh2, csc2)
            xc = xc + cgt2 * rmsnorm(conv_ffn(hc, w_up[l], conv_w[l], conv_b[l], w_down[l]), g_post_ffn[l])
    return x
```

```python
import math
from contextlib import ExitStack
import numpy as np
import concourse.bass as bass
import concourse.mybir as mybir
from concourse.bass_utils import run_bass_kernel_spmd

F32 = mybir.dt.float32
BF16 = mybir.dt.bfloat16
AF = mybir.ActivationFunctionType
ALU = mybir.AluOpType
AX = mybir.AxisListType

ENGS = ("pe", "act", "dve", "pool", "sp")


class Sched:
    def __init__(self, nc, stack, n_dma_sems=40):
        self.nc = nc
        self.esem = {e: stack.enter_context(nc.semaphore("s_" + e)) for e in ENGS}
        self.ecnt = {e: 0 for e in ENGS}
        self.dsem = [stack.enter_context(nc.semaphore("d%d" % i)) for i in range(n_dma_sems)]
        self.dcnt = [0] * n_dma_sems
        self.dsem_next = 0
        self.reset()

    def reset(self):
        self.ops = []
        self.lastw = {}
        self.readers = {}

    def new_dma_sem(self):
        i = self.dsem_next
        self.dsem_next += 1
        assert i < len(self.dsem), "out of DMA semaphores"
        return i

    def _add(self, eng, fn, reads, writes, dsem):
        i = len(self.ops)
        deps = set()
        for k in reads:
            w = self.lastw.get(k)
            if w is not None:
                deps.add(w)
        for k in writes:
            w = self.lastw.get(k)
            if w is not None:
                deps.add(w)
            for r in self.readers.get(k, ()):
                deps.add(r)
        deps.discard(i)
        for k in reads:
            self.readers.setdefault(k, []).append(i)
        for k in writes:
            self.lastw[k] = i
            self.readers[k] = []
        self.ops.append([eng, fn, deps, dsem, None])
        return i

    def op(self, eng, fn, reads=(), writes=()):
        return self._add(eng, fn, reads, writes, None)

    def dma(self, eng, fn, reads=(), writes=(), sem=None):
        assert sem is not None
        return self._add(eng, fn, reads, writes, sem)

    def flush(self):
        nc = self.nc
        ops = self.ops
        needed = set()
        for o in ops:
            eng, fn, deps, dsem, _ = o
            for d in deps:
                od = ops[d]
                if od[3] is None and od[0] == "pe" and eng == "pe" and dsem is None:
                    continue
                needed.add(d)
        last = {}
        for i, o in enumerate(ops):
            if o[3] is None:
                last[o[0]] = i
        for e, i in last.items():
            needed.add(i)
        for i, o in enumerate(ops):
            eng, fn, deps, dsem, _ = o
            if dsem is not None:
                self.dcnt[dsem] += 16
                o[4] = (("d", dsem), self.dcnt[dsem])
            elif i in needed:
                self.ecnt[eng] += 1
                o[4] = (("e", eng), self.ecnt[eng])
        final = {}
        for o in ops:
            if o[4] is not None:
                final[o[4][0]] = max(final.get(o[4][0], 0), o[4][1])
        per_eng = {e: [] for e in ENGS}
        for i, o in enumerate(ops):
            per_eng[o[0]].append(i)

        def semobj(sid):
            return self.esem[sid[1]] if sid[0] == "e" else self.dsem[sid[1]]

        def emit(eng_name, eng):
            waited = {}
            for i in per_eng[eng_name]:
                _, fn, deps, dsem, sig = ops[i]
                w = {}
                for d in deps:
                    s = ops[d][4]
                    if s is None:
                        continue
                    if eng_name == "pe" and ops[d][0] == "pe" and ops[d][3] is None and dsem is None:
                        continue
                    if s[1] > w.get(s[0], 0):
                        w[s[0]] = s[1]
                for sid, val in w.items():
                    if waited.get(sid, 0) >= val:
                        continue
                    waited[sid] = val
                    eng.wait_ge(semobj(sid), val)
                ins = fn(eng)
                if sig is not None:
                    ins.then_inc(semobj(sig[0]), 16 if sig[0][0] == "d" else 1)
            for sid, val in final.items():
                if waited.get(sid, 0) >= val:
                    continue
                eng.wait_ge(semobj(sid), val)

        with nc.Block() as block:
            @block.tensor
            def _(e):
                emit("pe", e)

            @block.scalar
            def _(e):
                emit("act", e)

            @block.vector
            def _(e):
                emit("dve", e)

            @block.gpsimd
            def _(e):
                emit("pool", e)

            @block.sync
            def _(e):
                emit("sp", e)
        n = len(ops)
        self.reset()
        self.dsem_next = 0
        return n


class Ring:
    def __init__(self, sch, tiles, name, dma=False):
        self.tiles = tiles
        self.keys = [(name, i) for i in range(len(tiles))]
        self.sems = [sch.new_dma_sem() for _ in tiles] if dma else [None] * len(tiles)
        self.i = -1

    def next(self):
        self.i = (self.i + 1) % len(self.tiles)
        return self.tiles[self.i], self.keys[self.i], self.sems[self.i]


NT, TL, CL, DM = 4352, 4096, 256, 1024
GROUPS = [(i * 512, 512) for i in range(8)] + [(4096, 256)]
EPS = 1e-6
DFF = 2816
NFF = DFF // 128
HC = 32
NCH = NT // HC
V_BADA, V_GPRE1, V_GPOST1, V_GPRE2, V_GPOST2, V_CW, V_CB, V_ATTG, V_HGG, V_LB = 0, 48, 56, 64, 72, 80, 212, 256, 257, 258
V_PER = 258 + 32 + 2
FC_Q, FC_K, FC_GU, FC_HQ, FC_HF, FC_HB, FC_HG, FC_GATE = 0, 16, 32, 40, 48, 56, 64, 72
N_FM = 96
TM_V, TM_GV, TM_HI = 0, 2, 4
W_EXT = N_FM * 128 + 6 * 512


class A:
    def __init__(self, sch):
        self.s = sch

    def act(self, out, in_, func, r, w, **kw):
        self.s.op("act", lambda e: e.activation(out=out, in_=in_, func=func, **kw), r, w)

    def tt(self, eng, out, in0, in1, op, r, w):
        self.s.op(eng, lambda e: e.tensor_tensor(out=out, in0=in0, in1=in1, op=op), r, w)

    def ts(self, eng, out, in0, s1, s2, op0, op1, r, w):
        self.s.op(eng, lambda e: e.tensor_scalar(out=out, in0=in0, scalar1=s1, scalar2=s2, op0=op0, op1=op1), r, w)

    def stt(self, out, in0, scalar, in1, op0, op1, r, w):
        self.s.op("dve", lambda e: e.scalar_tensor_tensor(out=out, in0=in0, scalar=scalar, in1=in1, op0=op0, op1=op1), r, w)

    def copy(self, eng, out, in_, r, w):
        if eng == "act":
            self.s.op("act", lambda e: e.copy(out=out, in_=in_), r, w)
        else:
            self.s.op(eng, lambda e: e.tensor_copy(out=out, in_=in_), r, w)

    def recip(self, out, in_, r, w):
        self.s.op("dve", lambda e: e.reciprocal(out=out, in_=in_), r, w)

    def memset(self, eng, ap, val, w):
        self.s.op(eng, lambda e: e.memset(ap, val), (), w)

    def mm(self, out, lhsT, rhs, start, stop, r, w):
        self.s.op("pe", lambda e: e.matmul(out, lhsT=lhsT, rhs=rhs, start=start, stop=stop), r, w)

    def tr(self, out, in_, ident, r, w):
        self.s.op("pe", lambda e: e.transpose(out, in_, ident), r, w)

    def ld(self, out, in_, w, sem, r=(), q="sp"):
        self.s.dma(q, lambda e: e.dma_start(out=out, in_=in_), r, w, sem)

    def st(self, out, in_, r, sem, w=(), q="sp"):
        self.s.dma(q, lambda e: e.dma_start(out=out, in_=in_), r, w, sem)


class Ctx:
    pass


class NCProxy:
    def __init__(self, nc):
        self._nc = nc
        self._n = 0

    def __getattr__(self, k):
        return getattr(self._nc, k)

    def sbuf_tensor(self, name, shape, dt):
        self._n += 1
        return self._nc.sbuf_tensor("%s_%d" % (name, self._n), shape, dt)

    def psum_tensor(self, name, shape, dt):
        self._n += 1
        return self._nc.psum_tensor("%s_%d" % (name, self._n), shape, dt)


def ps_alloc(C, st, with_bf16=False):
    n = 7 if with_bf16 else 8
    C.ps = [st.enter_context(C.nc.psum_tensor("ps%d" % i, [128, 512], F32)) for i in range(n)]
    C.psb = st.enter_context(C.nc.psum_tensor("psb", [128, 1024], BF16)) if with_bf16 else None


def sb_ring(C, st, name, shape, dt, n, dma=False):
    tiles = [st.enter_context(C.nc.sbuf_tensor("%s%d" % (name, i), shape, dt)) for i in range(n)]
    return Ring(C.sch, tiles, name, dma)


class PsRing:
    def __init__(self, tiles, name):
        self.tiles = tiles
        self.keys = [(name, i) for i in range(len(tiles))]
        self.i = -1

    def next(self):
        self.i = (self.i + 1) % len(self.tiles)
        return self.tiles[self.i], self.keys[self.i]


def fm_rstd(C, xt, kx, KC, W, nfeat, psn, kpsn, sq, ksq, tmp, ktmp, rstd, krstd):
    a = C.a
    a.act(sq[:, :KC, :W], xt, AF.Square, [kx], [ksq])
    for kc in range(KC):
        a.mm(psn[:, :W], C.onesb[:], sq[:, kc, :W], kc == 0, kc == KC - 1, [ksq], [kpsn])
    a.act(tmp[:, :W], psn[:, :W], AF.Sqrt, [kpsn], [ktmp], scale=1.0 / nfeat, bias=EPS)
    a.recip(rstd[:, :W], tmp[:, :W], [ktmp], [krstd])


def norm_mod(C, R, xt, kx, W, g0, l, which, dst=None):
    a = C.a
    j = 1 if g0 >= TL else 0
    gm = C.modv[:, l, 1 + 3 * which]
    sh = C.modv[:, l, 0 + 3 * which]
    sq, ksq, _ = R["sq"].next()
    psn, kpsn = R["psn"].next()
    tmp, ktmp, _ = R["ntmp"].next()
    rstd, krstd, _ = R["rstd"].next()
    fm_rstd(C, xt, kx, 8, W, DM, psn, kpsn, sq, ksq, tmp, ktmp, rstd, krstd)
    for kc in range(8):
        tf, ktf, _ = R["tmpf"].next()
        a.stt(tf[:, :W], xt[:, kc, :], gm[:, kc, j:j + 1], rstd[:, :W], ALU.mult, ALU.mult, [kx, krstd], [ktf])
        if dst is None:
            o_ap, o_key = C.hT[:, kc, g0:g0 + W], ("hT", kc, g0)
        else:
            o_ap, o_key = dst(kc)
        a.act(o_ap, tf[:, :W], AF.Identity, [ktf], [o_key], bias=sh[:, kc, j:j + 1], scale=1.0)


def norm_rings(C, st, W=512):
    R = {}
    R["sq"] = sb_ring(C, st, "nsq", [128, 8, W], BF16, 2)
    R["ntmp"] = sb_ring(C, st, "ntmp", [128, W], F32, 2)
    R["rstd"] = sb_ring(C, st, "nrstd", [128, W], F32, 2)
    R["tmpf"] = sb_ring(C, st, "ntf", [128, W], F32, 3)
    return R


def phase_prologue(C):
    nc, sch, a, Dm = C.nc, C.sch, C.a, C.D
    with ExitStack() as st:
        ps_alloc(C, st)
        sems = [sch.new_dma_sem() for _ in range(4)]
        a.memset("pool", C.ident[:], 1.0, ["ident"])
        sch.op("pool", lambda e: e.affine_select(out=C.ident[:], in_=C.ident[:], pattern=[[-1, 128]], compare_op=ALU.is_equal,
                                                 fill=0.0, base=0, channel_multiplier=1), ["ident"], ["ident"])
        a.copy("pool", C.identb[:], C.ident[:], ["ident"], ["identb"])
        a.memset("pool", C.onesb[:], 1.0, ["onesb"])
        for nm, sg_ in (("maskf", 1), ("maskb", -1)):
            m = getattr(C, nm)
            a.memset("pool", m[0:32, :], 1.0, [nm])
            sch.op("pool", lambda e, m=m, sg_=sg_: e.affine_select(
                out=m[0:32, :], in_=m[0:32, :], pattern=[[sg_, 32]], compare_op=ALU.is_ge,
                fill=0.0, base=0, channel_multiplier=-sg_), [nm], [nm])
            a.ld(m[32:64, :], m[0:32, :], [(nm, "hi")], sch.new_dma_sem(), r=[nm])
        a.ld(C.vec[:], Dm["vec"], ["vec"], sems[3])
        lamt = st.enter_context(nc.sbuf_tensor("lamt", [128, 2, 4, 64], F32))
        a.ld(lamt[:], Dm["lamv"], ["lamt"], sems[2])
        cc = st.enter_context(nc.sbuf_tensor("cc", [128, 8, 2], F32))
        a.ld(cc[:], Dm["cc"], ["cc"], sems[0])
        scc = st.enter_context(nc.sbuf_tensor("scc", [128, 8, 2], F32))
        a.act(scc[:], cc[:], AF.Silu, ["cc"], ["scc"])
        lt = st.enter_context(nc.sbuf_tensor("lt", [128, 2, 2, 64], F32))
        ls = st.enter_context(nc.sbuf_tensor("ls", [128, 4], F32))
        le = st.enter_context(nc.sbuf_tensor("le", [128, 4], F32))
        for l in range(2):
            for i in range(2):
                a.tt("dve", lt[:, l, i, :], lamt[:, l, 2 * i, :], lamt[:, l, 2 * i + 1, :], ALU.mult, ["lamt"], ["lt"])
                sch.op("dve", lambda e, l=l, i=i: e.reduce_sum(out=ls[:, 2 * l + i:2 * l + i + 1], in_=lt[:, l, i, :], axis=AX.X), ["lt"], ["ls"])
        a.act(le[:], ls[:], AF.Exp, ["ls"], ["le"])
        for l in range(2):
            lam_init = 0.8 - 0.6 * math.exp(-0.3 * l)
            a.stt(C.neglam[:, l:l + 1], le[:, 2 * l + 1:2 * l + 2], -lam_init, le[:, 2 * l:2 * l + 1], ALU.add, ALU.subtract, ["le"], ["neglam"])
            a.ts("dve", C.attg[:, l:l + 1], C.vec[:, l * V_PER + V_ATTG:l * V_PER + V_ATTG + 1], 1.0 - lam_init, None, ALU.mult, ALU.bypass,
                 ["vec"], ["attg"])
        a.memset("pool", C.lb[:, 0], 0.0, ["lb"])
        a.memset("pool", C.oml[:, 0], 1.0, ["oml"])
        lbd = st.enter_context(nc.sbuf_tensor("lbd", [128, 16], F32))
        a.tt("dve", lbd[:], C.vec[:, V_LB + 16:V_LB + 32], C.vec[:, V_LB:V_LB + 16], ALU.subtract, ["vec"], ["lbd"])
        a.act(C.lb[:, 1].rearrange("p d h -> p (d h)"), lbd[:], AF.Sigmoid, ["lbd"], ["lb"])
        a.ts("dve", C.oml[:, 1].rearrange("p d h -> p (d h)"), C.lb[:, 1].rearrange("p d h -> p (d h)"), -1.0, 1.0, ALU.mult, ALU.add, ["lb"], ["oml"])
        wring = sb_ring(C, st, "wada", [128, 8, 512], F32, 2, dma=True)
        modr = st.enter_context(nc.sbuf_tensor("modr", [128, 2, 48, 2], F32))
        psm = C.ps[0]
        for l in range(2):
            for fb in range(12):
                wt, kw, sw = wring.next()
                a.ld(wt[:], Dm["w_ada"][l].rearrange("(kc p) f -> p kc f", p=128)[:, :, fb * 512:(fb + 1) * 512], [kw], sw)
                for fi in range(4):
                    f = fb * 4 + fi
                    for kc in range(8):
                        a.mm(psm[:, 2 * f:2 * f + 2], wt[:, kc, fi * 128:(fi + 1) * 128], scc[:, kc, :], kc == 0, kc == 7, [kw, "scc"], ["psm"])
            for j in range(2):
                a.tt("dve", modr[:, l, :, j], psm[:, 0:96].rearrange("p (f j) -> p f j", j=2)[:, :, j],
                     C.vec[:, l * V_PER + V_BADA:l * V_PER + V_BADA + 48], ALU.add, ["psm", "vec"], ["modr"])
            for w2 in range(2):
                base = 24 * w2
                gpre = C.vec[:, l * V_PER + (V_GPRE1 if w2 == 0 else V_GPRE2):][:, 0:8]
                gpost = C.vec[:, l * V_PER + (V_GPOST1 if w2 == 0 else V_GPOST2):][:, 0:8]
                for j in range(2):
                    a.copy("dve", C.modv[:, l, 3 * w2 + 0, :, j], modr[:, l, base:base + 8, j], ["modr"], ["modv"])
                    a.stt(C.modv[:, l, 3 * w2 + 1, :, j], modr[:, l, base + 8:base + 16, j], 1.0, gpre, ALU.add, ALU.mult, ["modr", "vec"], ["modv"])
                    a.tt("dve", C.modv[:, l, 3 * w2 + 2, :, j], modr[:, l, base + 16:base + 24, j], gpost, ALU.mult, ["modr", "vec"], ["modv"])
        R = norm_rings(C, st)
        R["psn"] = PsRing([C.ps[1]], "psn")
        xin = sb_ring(C, st, "xin", [128, 1024], F32, 2, dma=True)
        xg = sb_ring(C, st, "xg", [128, 8, 512], F32, 2, dma=True)
        pst = PsRing([(C.ps[2], C.ps[3]), (C.ps[4], C.ps[5])], "pst")
        for (g0, W) in GROUPS:
            xgt, kxg, sxg = xg.next()
            for ti in range(W // 128):
                xt, kxt, sxt = xin.next()
                a.ld(xt[:], Dm["xin"][g0 + ti * 128:g0 + (ti + 1) * 128, :], [kxt], sxt)
                (pa, pb), kp = pst.next()
                for kc in range(8):
                    pp = pa if kc < 4 else pb
                    a.tr(pp[:, (kc % 4) * 128:(kc % 4 + 1) * 128], xt[:, kc * 128:(kc + 1) * 128], C.ident[:], [kxt, "ident"], [kp])
                a.copy("dve", xgt[:, 0:4, ti * 128:(ti + 1) * 128], pa[:].rearrange("p (k t) -> p k t", t=128), [kp], [kxg])
                a.copy("act", xgt[:, 4:8, ti * 128:(ti + 1) * 128], pb[:].rearrange("p (k t) -> p k t", t=128), [kp], [kxg])
            a.st(Dm["xT"].rearrange("(kc p) t -> p kc t", p=128)[:, :, g0:g0 + W], xgt[:, :, :W], [kxg], sxg)
            norm_mod(C, R, xgt[:, :, :W], kxg, W, g0, 0, 0)
        sch.flush()


def fmview(ap):
    return ap.rearrange("(fc p) t -> p fc t", p=128)


def phase_inproj(C, l):
    nc, sch, a, Dm = C.nc, C.sch, C.a, C.D
    with ExitStack() as st:
        ps_alloc(C, st)
        ropeC = st.enter_context(nc.sbuf_tensor("ropeC", [128, NT], F32))
        ropeS = st.enter_context(nc.sbuf_tensor("ropeS", [128, NT], F32))
        s0 = sch.new_dma_sem()
        a.ld(ropeC[:], Dm["ropeC"], ["ropeC"], s0)
        a.ld(ropeS[:], Dm["ropeS"], ["ropeS"], sch.new_dma_sem())
        wst = sb_ring(C, st, "wst", [128, 8, 512], F32, 2, dma=True)
        wbf = sb_ring(C, st, "wbf", [128, 8, 512], BF16, 2)
        o32 = sb_ring(C, st, "o32", [128, 512], F32, 6, dma=True)
        o16 = sb_ring(C, st, "o16", [128, 512], BF16, 4, dma=True)
        t32 = sb_ring(C, st, "t32", [128, 512], F32, 6)
        psr = PsRing(C.ps[0:6], "psA")
        wsrc = Dm["w_in"][l].rearrange("(kc p) f -> p kc f", p=128)
        nblk = W_EXT // 512
        lbv, omlv = C.lb, C.oml

        def load_w(b):
            wt, kw, sw = wst.next()
            a.ld(wt[:], wsrc[:, :, b * 512:(b + 1) * 512], [kw], sw)
            wb, kb, _ = wbf.next()
            a.copy("pool", wb[:], wt[:], [kw], [kb])
            return wb, kb

        def fm_mm(wb, kb, fi, g0, W):
            p, kp = psr.next()
            for kc in range(8):
                a.mm(p[:, :W], wb[:, kc, fi * 128:(fi + 1) * 128], C.hT[:, kc, g0:g0 + W], kc == 0, kc == 7, [kb], [kp])
            return p, kp

        nxt = load_w(0)
        for b in range(nblk):
            wb, kb = nxt
            if b + 1 < nblk:
                nxt = load_w(b + 1)
            if b < N_FM // 4:
                for (g0, W) in GROUPS:
                    fcs = [b * 4 + i for i in range(4)]
                    if fcs[0] < FC_GU:
                        for pi in range(2):
                            fc = fcs[2 * pi]
                            p1, kp1 = fm_mm(wb, kb, 2 * pi, g0, W)
                            p2, kp2 = fm_mm(wb, kb, 2 * pi + 1, g0, W)
                            t1, kt1, _ = t32.next()
                            t2, kt2, _ = t32.next()
                            a.tt("dve", t1[:, :W], p1[:, :W], ropeC[:, g0:g0 + W], ALU.mult, [kp1, "ropeC"], [kt1])
                            a.tt("dve", t2[:, :W], p2[:, :W], ropeS[:, g0:g0 + W], ALU.mult, [kp2, "ropeS"], [kt2])
                            o, ko, so = o16.next()
                            a.tt("pool", o[:, :W], t1[:, :W], t2[:, :W], ALU.add, [kt1, kt2], [ko])
                            isk = fc >= FC_K
                            hd = ((fc - FC_K) if isk else fc) // 2
                            dst = Dm["kT" if isk else "qT"]
                            a.st(dst[hd * 128:(hd + 1) * 128, g0:g0 + W], o[:, :W], [ko], so)
                        continue
                    for fi, fc in enumerate(fcs):
                        p, kp = fm_mm(wb, kb, fi, g0, W)
                        if fc < FC_HQ:
                            o, ko, so = o32.next()
                            a.act(o[:, :W], p[:, :W], AF.Gelu_apprx_tanh, [kp], [ko])
                            a.st(Dm["guT"][(fc - FC_GU) * 128:(fc - FC_GU + 1) * 128, g0:g0 + W], o[:, :W], [ko], so)
                        elif fc < FC_HF:
                            o, ko, so = o32.next()
                            a.act(o[:, :W], p[:, :W], AF.Silu, [kp], [ko])
                            a.st(Dm["hqT"][(fc - FC_HQ) * 128:(fc - FC_HQ + 1) * 128, g0:g0 + W], o[:, :W], [ko], so)
                        elif fc < FC_HG:
                            d = 0 if fc < FC_HB else 1
                            hd = fc - (FC_HF if d == 0 else FC_HB)
                            t1, kt1, _ = t32.next()
                            a.act(t1[:, :W], p[:, :W], AF.Sigmoid, [kp], [kt1])
                            t2, kt2, _ = t32.next()
                            a.ts("dve", t2[:, :W], t1[:, :W], omlv[:, l, d, hd:hd + 1], lbv[:, l, d, hd:hd + 1], ALU.mult, ALU.add, [kt1], [kt2])
                            o, ko, so = o32.next()
                            a.act(o[:, :W], t2[:, :W], AF.Ln, [kt2], [ko])
                            a.st(Dm["lfT"][d, hd * 128:(hd + 1) * 128, g0:g0 + W], o[:, :W], [ko], so)
                            o2, ko2, so2 = o32.next()
                            a.ts("pool", o2[:, :W], t2[:, :W], -1.0, 1.0, ALU.mult, ALU.add, [kt2], [ko2])
                            a.st(Dm["kgT"][d, hd * 128:(hd + 1) * 128, g0:g0 + W], o2[:, :W], [ko2], so2)
                        elif fc < FC_GATE:
                            o, ko, so = o32.next()
                            a.act(o[:, :W], p[:, :W], AF.Silu, [kp], [ko])
                            a.st(Dm["hgT"][(fc - FC_HG) * 128:(fc - FC_HG + 1) * 128, g0:g0 + W], o[:, :W], [ko], so)
                        else:
                            o, ko, so = o32.next()
                            a.act(o[:, :W], p[:, :W], AF.Sigmoid, [kp], [ko])
                            a.st(Dm["gatesT"][(fc - FC_GATE) * 128:(fc - FC_GATE + 1) * 128, g0:g0 + W], o[:, :W], [ko], so)
            else:
                tb = b - N_FM // 4
                fam, half = tb // 2, tb % 2
                for tt_ in range(NT // 128):
                    p, kp = psr.next()
                    for kc in range(8):
                        a.mm(p[:, :], C.hT[:, kc, tt_ * 128:(tt_ + 1) * 128], wb[:, kc, :], kc == 0, kc == 7, [kb], [kp])
                    rows = slice(tt_ * 128, (tt_ + 1) * 128)
                    cols = slice(half * 512, (half + 1) * 512)
                    if fam == 1:
                        o, ko, so = o32.next()
                        a.act(o[:, :], p[:, :], AF.Gelu_apprx_tanh, [kp], [ko])
                        a.st(Dm["gvg"][rows, cols], o[:, :], [ko], so)
                    else:
                        o, ko, so = o16.next()
                        a.copy("dve" if tt_ % 2 == 0 else "act", o[:, :], p[:, :], [kp], [ko])
                        a.st(Dm["Vt" if fam == 0 else "hv"][rows, cols], o[:, :], [ko], so)
        sch.flush()


def phase_attn(C, l):
    nc, sch, a, Dm = C.nc, C.sch, C.a, C.D
    last = (l == 1)
    with ExitStack() as st:
        ps_alloc(C, st)
        kTr = sb_ring(C, st, "akT", [128, NT], BF16, 2, dma=True)
        qTr = [sb_ring(C, st, "aqT%d" % i, [128, NT], BF16, 2, dma=True) for i in range(2)]
        for i in range(2):
            for t_, k_ in zip(qTr[i].tiles, qTr[i].keys):
                a.memset("pool", t_[(1 - i) * 64:(2 - i) * 64, :], 0.0, [k_])
        Vr = sb_ring(C, st, "aV", [128, 34, 128], BF16, 2, dma=True)
        pTr = sb_ring(C, st, "apT", [128, 512], BF16, 6)
        er = sb_ring(C, st, "ae", [128, 512], F32, 22)
        sqr = sb_ring(C, st, "asq", [128, 512], BF16, 3)
        yr = sb_ring(C, st, "ay", [128, 512], BF16, 2, dma=True)
        pss = PsRing([C.ps[0], C.ps[1], C.ps[6]], "ps_s")
        pso, kpso = [C.ps[2], C.ps[3]], ["pso0", "pso1"]
        psd, kpsd = [C.ps[4], C.ps[5]], ["psd0", "psd1"]
        groups = GROUPS[:8] if last else GROUPS
        Vsrc = Dm["Vt"].rearrange("(kt p) f -> p kt f", p=128)

        def load_head(hd):
            kt_, kk, sk = kTr.next()
            a.ld(kt_[:], Dm["kT"][hd * 128:(hd + 1) * 128, :], [kk], sk)
            qt_, kq = [], []
            for i in range(2):
                t_, k_, s_ = qTr[i].next()
                a.ld(t_[i * 64:(i + 1) * 64, :], Dm["qT"][hd * 128 + i * 64:hd * 128 + (i + 1) * 64, :], [k_], s_)
                qt_.append(t_)
                kq.append(k_)
            v_, kv, sv = Vr.next()
            a.ld(v_[:], Vsrc[:, :, hd * 128:(hd + 1) * 128], [kv], sv)
            return (kt_, kk, qt_, kq, v_, kv)

        nxt = load_head(0)
        deferred = []
        for hd in range(8):
            kt_, kk, qt_, kq, v_, kv = nxt
            if hd + 1 < 8:
                nxt = load_head(hd + 1)
            for (g0, W) in groups:
                kts = list(range(34)) if g0 < TL else [32, 33]
                its = [(kt, sub) for kt in kts for sub in range(2)]
                LA = 2
                pend = []
                for i in range(len(its) + LA):
                    if deferred and (i == 28 or i == len(its) + LA - 1):
                        for f in deferred:
                            f()
                        del deferred[:]
                    if i < len(its):
                        kt, sub = its[i]
                        s, ks = pss.next()
                        a.mm(s[:, :W], kt_[:, kt * 128:(kt + 1) * 128], qt_[sub][:, g0:g0 + W], True, True, [kk, kq[sub]], [ks])
                        pt, kpt, _ = pTr.next()
                        a.act(pt[:, :W], s[:, :W], AF.Exp, [ks], [kpt], scale=0.125)
                        pend.append((pt, kpt))
                    if i >= LA:
                        kt, sub = its[i - LA]
                        pt, kpt = pend[i - LA]
                        a.mm(pso[sub][:, :W], v_[:, kt, :], pt[:, :W], kt == kts[0], kt == kts[-1], [kv, kpt], [kpso[sub]])
                        a.mm(psd[sub][:, :W], C.onesb[:], pt[:, :W], kt == kts[0], kt == kts[-1], [kpt], [kpsd[sub]])
                cp = []
                for src, ksrc in ((psd[0], kpsd[0]), (pso[0], kpso[0]), (psd[1], kpsd[1]), (pso[1], kpso[1])):
                    t_, k_, _ = er.next()
                    a.copy("dve", t_[:, :W], src[:, :W], [ksrc], [k_])
                    cp.append((t_, k_))
                r1, kr1, _ = er.next()
                a.recip(r1[:, :W], cp[0][0][:, :W], [cp[0][1]], [kr1])
                o1, ko1, _ = er.next()
                a.tt("pool", o1[:, :W], cp[1][0][:, :W], r1[:, :W], ALU.mult, [cp[1][1], kr1], [ko1])
                r2, kr2, _ = er.next()
                a.recip(r2[:, :W], cp[2][0][:, :W], [cp[2][1]], [kr2])
                o2, ko2, _ = er.next()
                a.tt("pool", o2[:, :W], cp[3][0][:, :W], r2[:, :W], ALU.mult, [cp[3][1], kr2], [ko2])
                o, ko, _ = er.next()
                a.stt(o[:, :W], o2[:, :W], C.neglam[:, l:l + 1], o1[:, :W], ALU.mult, ALU.add, [ko1, ko2], [ko])

                def part_b(o=o, ko=ko, W=W, hd=hd, g0=g0):
                    sq, ksq, _ = sqr.next()
                    a.act(sq[:, :W], o[:, :W], AF.Square, [ko], [ksq])
                    pn, kpn = pss.next()
                    a.mm(pn[:, :W], C.onesb[:], sq[:, :W], True, True, [ksq], [kpn])
                    tm, ktm, _ = er.next()
                    a.act(tm[:, :W], pn[:, :W], AF.Sqrt, [kpn], [ktm], scale=1.0 / 128, bias=EPS)
                    rs, krs, _ = er.next()
                    a.recip(rs[:, :W], tm[:, :W], [ktm], [krs])
                    y, ky, sy = yr.next()
                    a.stt(y[:, :W], o[:, :W], C.attg[:, l:l + 1], rs[:, :W], ALU.mult, ALU.mult, [ko, krs], [ky])
                    a.st(Dm["yattT"][hd * 128:(hd + 1) * 128, g0:g0 + W], y[:, :W], [ky], sy)

                deferred.append(part_b)
        for f in deferred:
            f()
        sch.flush()


def phase_gmlp(C, l):
    nc, sch, a, Dm = C.nc, C.sch, C.a, C.D
    with ExitStack() as st:
        ps_alloc(C, st)
        s0 = sch.new_dma_sem()
        wsf = st.enter_context(nc.sbuf_tensor("gwsf", [128, 8, 128], F32))
        wsb = st.enter_context(nc.sbuf_tensor("gwsb", [128, 8, 128], BF16))
        bsbc = st.enter_context(nc.sbuf_tensor("gbsbc", [128, 8, 128], F32))
        lng = st.enter_context(nc.sbuf_tensor("glng", [128, 1024], F32))
        lnb = st.enter_context(nc.sbuf_tensor("glnb", [128, 1024], F32))
        a.ld(wsf[:], Dm["gm_wsT"][l], ["gwsf"], s0)
        a.ld(bsbc[:], Dm["gm_bsbc"][l], ["gbsbc"], sch.new_dma_sem())
        a.ld(lng[:], Dm["gm_lng"][l], ["glng"], sch.new_dma_sem())
        a.ld(lnb[:], Dm["gm_lnb"][l], ["glnb"], sch.new_dma_sem())
        a.copy("dve", wsb[:], wsf[:], ["gwsf"], ["gwsb"])
        gvr = sb_ring(C, st, "ggv", [128, 1024], F32, 2, dma=True)
        gur = sb_ring(C, st, "ggu", [128, 8, 128], F32, 2, dma=True)
        junk = st.enter_context(nc.sbuf_tensor("gjunk", [128, 1024], F32))
        str_ = sb_ring(C, st, "gst", [128, 8], F32, 2)
        t1r = sb_ring(C, st, "gt1", [128, 1024], F32, 2)
        vnr = sb_ring(C, st, "gvn", [128, 1024], BF16, 2)
        s1r = sb_ring(C, st, "gs1", [128, 8, 128], F32, 2)
        yr = sb_ring(C, st, "gy", [128, 8, 128], BF16, 2, dma=True)
        psr = PsRing([(C.ps[0], C.ps[1]), (C.ps[2], C.ps[3])], "gps")
        gusrc = Dm["guT"].rearrange("(g p) t -> p g t", p=128)
        ydst = Dm["ygmT"].rearrange("(g p) t -> p g t", p=128)

        def load(n):
            gv, kgv, sgv = gvr.next()
            a.ld(gv[:], Dm["gvg"][n * 128:(n + 1) * 128, :], [kgv], sgv)
            gu, kgu, sgu = gur.next()
            a.ld(gu[:], gusrc[:, :, n * 128:(n + 1) * 128], [kgu], sgu)
            return gv, kgv, gu, kgu

        nxt = load(0)
        for n in range(NT // 128):
            gv, kgv, gu, kgu = nxt
            if n + 1 < NT // 128:
                nxt = load(n + 1)
            s, ks, _ = str_.next()
            sch.op("dve", lambda e, s=s, gv=gv: e.reduce_sum(out=s[:, 0:1], in_=gv[:], axis=AX.X), [kgv], [ks])
            a.act(junk[:], gv[:], AF.Square, [kgv, ks], ["gjunk", ks], accum_out=s[:, 1:2])
            a.ts("dve", s[:, 2:3], s[:, 0:1], 1.0 / 1024, None, ALU.mult, ALU.bypass, [ks], [ks])
            a.tt("dve", s[:, 3:4], s[:, 2:3], s[:, 2:3], ALU.mult, [ks], [ks])
            a.stt(s[:, 4:5], s[:, 1:2], 1.0 / 1024, s[:, 3:4], ALU.mult, ALU.subtract, [ks], [ks])
            a.act(s[:, 5:6], s[:, 4:5], AF.Sqrt, [ks], [ks], scale=1.0, bias=EPS)
            a.recip(s[:, 6:7], s[:, 5:6], [ks], [ks])
            t1, kt1, _ = t1r.next()
            a.ts("dve", t1[:], gv[:], s[:, 2:3], s[:, 6:7], ALU.subtract, ALU.mult, [kgv, ks], [kt1])
            a.tt("pool", t1[:], t1[:], lng[:], ALU.mult, [kt1, "glng"], [kt1])
            vn, kvn, _ = vnr.next()
            a.tt("dve", vn[:], t1[:], lnb[:], ALU.add, [kt1, "glnb"], [kvn])
            (pa, pb), kp = psr.next()
            for g in range(8):
                pp = pa if g < 4 else pb
                a.mm(pp[:, (g % 4) * 128:(g % 4 + 1) * 128], vn[:, g * 128:(g + 1) * 128], wsb[:, g, :], True, True, [kvn, "gwsb"], [kp])
            s1, ks1, _ = s1r.next()
            a.tt("dve", s1[:, 0:4, :], pa[:].rearrange("p (g t) -> p g t", t=128), bsbc[:, 0:4, :], ALU.add, [kp, "gbsbc"], [ks1])
            a.tt("dve", s1[:, 4:8, :], pb[:].rearrange("p (g t) -> p g t", t=128), bsbc[:, 4:8, :], ALU.add, [kp, "gbsbc"], [ks1])
            y, ky, sy = yr.next()
            a.tt("pool", y[:], s1[:], gu[:], ALU.mult, [ks1, kgu], [ky])
            a.st(ydst[:, :, n * 128:(n + 1) * 128], y[:], [ky], sy)
        sch.flush()


SEG = 2176
NCS = SEG // HC


def phase_hgrn_prep(C, l):
    nc, sch, a, Dm = C.nc, C.sch, C.a, C.D
    with ExitStack() as st:
        ps_alloc(C, st, with_bf16=True)
        rmask = st.enter_context(nc.sbuf_tensor("hrmask", [128, SEG], F32))
        a.memset("pool", rmask[:], 1.0, ["hrmask"])
        a.memset("pool", rmask[:].rearrange("p (c t) -> p c t", t=HC)[:, :, 0:1], 0.0, ["hrmask"])
        lfr = sb_ring(C, st, "hlf", [128, SEG], F32, 2, dma=True)
        kgr = sb_ring(C, st, "hkg", [128, SEG], F32, 2, dma=True)
        qr = sb_ring(C, st, "hq", [128, SEG], F32, 2, dma=True)
        Ar = sb_ring(C, st, "hA", [128, SEG], F32, 2)
        Dr = sb_ring(C, st, "hD", [128, SEG], F32, 2)
        Er = sb_ring(C, st, "hE", [128, SEG], F32, 2)
        qor = sb_ring(C, st, "hqo", [128, SEG], BF16, 2, dma=True)
        kor = sb_ring(C, st, "hko", [128, SEG], BF16, 2, dma=True)
        ktr = sb_ring(C, st, "hkt", [128, 8, 128], BF16, 2, dma=True)
        scr = sb_ring(C, st, "hsc", [128, 4, NCS], F32, 2, dma=True)
        psb = PsRing([C.psb], "psb")
        units = [(hd, sg, d) for hd in range(8) for sg in range(2) for d in range(2)]
        qcur = [None]

        def load_unit(u):
            hd, sg, d = u
            t0 = sg * SEG
            if d == 0:
                q, kq, sq_ = qr.next()
                a.ld(q[:], Dm["hqT"][hd * 128:(hd + 1) * 128, t0:t0 + SEG], [kq], sq_)
                qcur[0] = (q, kq)
            lf, klf, slf = lfr.next()
            a.ld(lf[:], Dm["lfT"][d, hd * 128:(hd + 1) * 128, t0:t0 + SEG], [klf], slf)
            kg, kkg, skg = kgr.next()
            a.ld(kg[:], Dm["kgT"][d, hd * 128:(hd + 1) * 128, t0:t0 + SEG], [kkg], skg)
            return qcur[0] + (lf, klf, kg, kkg)

        nxtU = load_unit(units[0])
        for ui, (hd, sg, d) in enumerate(units):
            if True:
                t0 = sg * SEG
                if True:
                    q, kq, lf, klf, kg, kkg = nxtU
                    if ui + 1 < len(units):
                        nxtU = load_unit(units[ui + 1])
                    A_, kA, _ = Ar.next()
                    sch.op("dve", lambda e, A_=A_, lf=lf: e.tensor_tensor_scan(out=A_[:], data0=rmask[:], data1=lf[:], initial=0.0,
                                                                             op0=ALU.mult, op1=ALU.add), ["hrmask", klf], [kA])
                    A3 = A_[:].rearrange("p (c t) -> p c t", t=HC)
                    if d == 1:
                        D0, kD0, _ = Dr.next()
                        a.tt("pool", D0[:], lf[:], A_[:], ALU.subtract, [klf, kA], [kD0])
                        A2, kA2, _ = Ar.next()
                        a.tt("dve", A2[:].rearrange("p (c t) -> p c t", t=HC), D0[:].rearrange("p (c t) -> p c t", t=HC),
                             A3[:, :, HC - 1:HC].broadcast_to([128, NCS, HC]), ALU.add, [kD0, kA], [kA2])
                        A_, kA = A2, kA2
                        A3 = A_[:].rearrange("p (c t) -> p c t", t=HC)
                        iref, ilast = 16, 0
                    else:
                        iref, ilast = 15, HC - 1
                    Dd, kD, _ = Dr.next()
                    a.tt("dve", Dd[:].rearrange("p (c t) -> p c t", t=HC), A3, A3[:, :, iref:iref + 1].broadcast_to([128, NCS, HC]),
                         ALU.subtract, [kA], [kD])
                    sc, ksc, ssc = scr.next()
                    a.act(sc[:, 0, :], A3[:, :, ilast], AF.Exp, [kA], [ksc])
                    a.act(sc[:, 1, :], Dd[:].rearrange("p (c t) -> p c t", t=HC)[:, :, ilast], AF.Exp, [kD], [ksc])
                    a.act(sc[:, 2, :], A3[:, :, iref], AF.Exp, [kA], [ksc])
                    a.st(Dm["hsc"][d, hd, sg], sc[:], [ksc], ssc)
                    E1, kE1, _ = Er.next()
                    a.act(E1[:], Dd[:], AF.Exp, [kD], [kE1])
                    qo, kqo, sqo = qor.next()
                    a.tt("pool", qo[:], q[:], E1[:], ALU.mult, [kq, kE1], [kqo])
                    a.st(Dm["qtil"][d, hd * 128:(hd + 1) * 128, t0:t0 + SEG], qo[:], [kqo], sqo)
                    E2, kE2, _ = Er.next()
                    a.act(E2[:], Dd[:], AF.Exp, [kD], [kE2], scale=-1.0)
                    ko, kko, sko = kor.next()
                    a.tt("dve", ko[:], kg[:], E2[:], ALU.mult, [kkg, kE2], [kko])
                    a.st(Dm["ktilT"][d, hd * 128:(hd + 1) * 128, t0:t0 + SEG], ko[:], [kko], sko)
                    nb = SEG // 128
                    ktdst = Dm["ktok"][d, hd].rearrange("(b p) k -> p b k", p=128)
                    for b0 in range(0, nb, 8):
                        n = min(8, nb - b0)
                        pb_, kpb = psb.next()
                        for i in range(n):
                            a.tr(pb_[:, i * 128:(i + 1) * 128], ko[:, (b0 + i) * 128:(b0 + i + 1) * 128], C.identb[:], [kko, "identb"], [kpb])
                        kt_, kkt, skt = ktr.next()
                        a.copy("act" if (b0 // 8) % 2 == 0 else "dve", kt_[:, :n, :], pb_[:, :n * 128].rearrange("p (b k) -> p b k", k=128), [kpb], [kkt])
                        a.st(ktdst[:, t0 // 128 + b0:t0 // 128 + b0 + n, :], kt_[:, :n, :], [kkt], skt)
        sch.flush()


def phase_hgrn_scan(C, l):
    nc, sch, a, Dm = C.nc, C.sch, C.a, C.D
    NB = NT // 128
    with ExitStack() as st:
        ps_alloc(C, st)
        s_ld = [sch.new_dma_sem() for _ in range(12)]
        bm = [st.enter_context(nc.sbuf_tensor("sbm%d" % d, [128, 512], F32)) for d in range(2)]
        rmk = st.enter_context(nc.sbuf_tensor("srmk", [128, 4], F32))
        for d in range(2):
            a.ld(bm[d][:], Dm["hmask"][d], [("sbm", d)], s_ld[9 + d])
        a.ld(rmk[:], Dm["hrmk"], ["srmk"], s_ld[11])
        qtr = [sb_ring(C, st, "sq%d" % d, [128, NT], BF16, 2, dma=True) for d in range(2)]
        kTr = [sb_ring(C, st, "sk%d" % d, [128, NT], BF16, 2, dma=True) for d in range(2)]
        ktkr = [sb_ring(C, st, "skt%d" % d, [128, NB, 128], BF16, 2, dma=True) for d in range(2)]
        hsr = [sb_ring(C, st, "shs%d" % d, [128, 2, 4, NCS], F32, 2, dma=True) for d in range(2)]
        vtkr = sb_ring(C, st, "svt", [128, NB, 128], BF16, 2, dma=True)
        scT = [st.enter_context(nc.sbuf_tensor("sscT%d" % d, [128, NB, 128], BF16)) for d in range(2)]
        vmk = [st.enter_context(nc.sbuf_tensor("svm%d" % j, [128, NB, 128], BF16)) for j in range(4)]
        Sr = [sb_ring(C, st, "sS%d" % d, [128, 128], F32, 3) for d in range(2)]
        Srefr = sb_ring(C, st, "sSref", [128, 128], BF16, 6)
        tmpr = sb_ring(C, st, "stmp", [128, 128], F32, 6)
        oor = sb_ring(C, st, "soo", [128, 512], F32, 4, dma=True)
        pssc = PsRing([C.ps[0], C.ps[1]], "psbank01")
        pssc.keys = [("psbank", 0), ("psbank", 1)]
        ps_ub = [[(C.ps[0], ("psbank", 0)), (C.ps[6], ("psbank", 6))], [(C.ps[1], ("psbank", 1)), (C.ps[7], ("psbank", 7))]]
        ps_o = [[C.ps[2], C.ps[3]], [C.ps[4], C.ps[5]]]
        hvsrc = Dm["hv"].rearrange("(b p) f -> p b f", p=128)
        ucnt = [0, 0]

        def load_head(hd):
            H = dict(qt=[], kT=[], ktk=[], hs=[], kq=[], kk=[], kkt=[], khs=[])
            for d in range(2):
                t_, k_, s_ = qtr[d].next()
                a.ld(t_[:], Dm["qtil"][d, hd * 128:(hd + 1) * 128, :], [k_], s_)
                H["qt"].append(t_); H["kq"].append(k_)
                t_, k_, s_ = kTr[d].next()
                a.ld(t_[:], Dm["ktilT"][d, hd * 128:(hd + 1) * 128, :], [k_], s_)
                H["kT"].append(t_); H["kk"].append(k_)
                t_, k_, s_ = ktkr[d].next()
                a.ld(t_[:], Dm["ktok"][d, hd].rearrange("(b p) k -> p b k", p=128), [k_], s_)
                H["ktk"].append(t_); H["kkt"].append(k_)
                t_, k_, s_ = hsr[d].next()
                a.ld(t_[:], Dm["hsc"][d, hd].rearrange("s p a c -> p s a c"), [k_], s_)
                H["hs"].append(t_); H["khs"].append(k_)
            t_, k_, s_ = vtkr.next()
            a.ld(t_[:], hvsrc[:, :, hd * 128:(hd + 1) * 128], [k_], s_)
            H["vtk"], H["kv"] = t_, k_
            return H

        nxtH = load_head(0)
        for hd in range(8):
            H = nxtH
            qt, kT, ktk, hs, vtk = H["qt"], H["kT"], H["ktk"], H["hs"], H["vtk"]
            for j in range(4):
                if j % 2 == 0:
                    a.ts("dve", vmk[j][:], vtk[:], rmk[:, j:j + 1], None, ALU.mult, ALU.bypass, [H["kv"], "srmk"], [("svm", j)])
                else:
                    a.act(vmk[j][:], vtk[:], AF.Copy, [H["kv"], "srmk"], [("svm", j)], scale=rmk[:, j:j + 1])
            S, kS = [None, None], [None, None]
            for d in range(2):
                S[d], kS[d], _ = Sr[d].next()
                a.memset("pool", S[d][:], 0.0, [kS[d]])
            for d in range(2):
                for b0 in range(0, NB, 4):
                    n = min(4, NB - b0)
                    p, kp = pssc.next()
                    for i in range(n):
                        b = b0 + i
                        a.mm(p[:, i * 128:(i + 1) * 128], kT[d][:, b * 128:(b + 1) * 128], qt[d][:, b * 128:(b + 1) * 128], True, True,
                             [H["kk"][d], H["kq"][d]], [kp])
                    a.tt("dve", scT[d][:, b0:b0 + n, :], p[:, :n * 128].rearrange("p (b t) -> p b t", t=128),
                         bm[d][:, :n * 128].rearrange("p (b t) -> p b t", t=128), ALU.mult, [kp, ("sbm", d)], [("sscT", d, b0)])
            if hd + 1 < 8:
                nxtH = load_head(hd + 1)
            _dbg = 9
            orders = [list(range(128, 136)) + list(range(0, 128)), list(range(135, 127, -1)) + list(range(127, -1, -1))]
            ogrp = [{}, {}]
            pend_u = [None, None]

            _skip = []

            def emit_u(d, c):
                if "u" in _skip:
                    pend_u[d] = (S[d], kS[d])
                    return
                b, j = c // 4, c % 4
                pbank, kpu = ps_ub[d][ucnt[d] % 2]
                pu = pbank[:, 0:128]
                ucnt[d] += 1
                _v = ""
                if _v == "vtk":
                    a.mm(pu, ktk[d][:, b, :], vtk[:, b, :], True, True, [H["kkt"][d], H["kv"]], [kpu])
                elif _v == "kT":
                    a.mm(pu, kT[d][:, b * 128:(b + 1) * 128], vmk[j][:, b, :], True, True, [H["kk"][d], ("svm", j)], [kpu])
                else:
                    a.mm(pu, ktk[d][:, b, :], vmk[j][:, b, :], True, True, [H["kkt"][d], ("svm", j)], [kpu])
                sgi, ci = c // NCS, c % NCS
                tmp, ktmp, _ = tmpr.next()
                if _dbg == 2:
                    a.act(tmp[:], pu, AF.Copy, [kpu, H["khs"][d]], [ktmp], scale=hs[d][:, sgi, 1, ci:ci + 1])
                else:
                    a.ts("dve", tmp[:], pu, hs[d][:, sgi, 1, ci:ci + 1], None, ALU.mult, ALU.bypass, [kpu, H["khs"][d]], [ktmp])
                pend_u[d] = (tmp, ktmp)

            for d in range(2):
                emit_u(d, orders[d][0])
            for step in range(NCH):
                srefs = []
                for d in range(2):
                    c = orders[d][step]
                    sgi, ci = c // NCS, c % NCS
                    Sref, kSref, _ = Srefr.next()
                    if "sref" not in _skip:
                        a.act(Sref[:], S[d][:], AF.Copy, [kS[d], H["khs"][d]], [kSref], scale=hs[d][:, sgi, 2, ci:ci + 1])
                    srefs.append((Sref, kSref))
                cur_u = list(pend_u)
                for d in range(2):
                    c = orders[d][step]
                    cs = slice(c * HC, (c + 1) * HC)
                    gi, so = c // 16, (c % 16) * HC
                    nin = 8 if gi == 8 else 16
                    if _dbg <= 2:
                        continue
                    if gi not in ogrp[d]:
                        bank = ps_o[d][len(ogrp[d]) % 2]
                        kbank = ("pso", d, len(ogrp[d]) % 2)
                        ogrp[d][gi] = [bank, kbank, 0]
                        blks = list(range(4 * gi, min(4 * gi + 4, NB)))
                        for i, b in enumerate(blks):
                            a.mm(bank[:, i * 128:(i + 1) * 128], vtk[:, b, :], scT[d][:, b, :], i == 0, False,
                                 [H["kv"], ("sscT", d, 4 * gi)], [kbank])
                    bank, kbank, _ = ogrp[d][gi]
                    ogrp[d][gi][2] += 1
                    Sref, kSref = srefs[d]
                    if _dbg >= 4:
                        a.mm(bank[:, so:so + HC], Sref[:], qt[d][:, cs], False, ogrp[d][gi][2] == nin, [kSref, H["kq"][d]], [kbank])
                    if ogrp[d][gi][2] == nin:
                        Wg = nin * HC
                        oo, koo, soo = oor.next()
                        a.copy("act" if d == 0 else "dve", oo[:, :Wg], bank[:, :Wg], [kbank], [koo])
                        a.st(Dm["oT"][d, hd * 128:(hd + 1) * 128, gi * 512:gi * 512 + Wg], oo[:, :Wg], [koo], soo)
                for d in range(2):
                    c = orders[d][step]
                    sgi, ci = c // NCS, c % NCS
                    tmp, ktmp = cur_u[d]
                    if "stt" in _skip:
                        continue
                    Sn, kSn, _ = Sr[d].next()
                    a.stt(Sn[:], S[d][:], hs[d][:, sgi, 0, ci:ci + 1], tmp[:], ALU.mult, ALU.add, [kS[d], ktmp, H["khs"][d]], [kSn])
                    S[d], kS[d] = Sn, kSn
                if step + 1 < NCH:
                    for d in range(2):
                        emit_u(d, orders[d][step + 1])
        sch.flush()


def phase_hgrn_out(C, l):
    nc, sch, a, Dm = C.nc, C.sch, C.a, C.D
    with ExitStack() as st:
        ps_alloc(C, st)
        ofr = sb_ring(C, st, "of", [128, 512], F32, 2, dma=True)
        obr = sb_ring(C, st, "ob", [128, 512], F32, 2, dma=True)
        hgr = sb_ring(C, st, "ohg", [128, 512], F32, 2, dma=True)
        er = sb_ring(C, st, "oe", [128, 512], F32, 6)
        sqr = sb_ring(C, st, "osq", [128, 512], BF16, 2)
        yr = sb_ring(C, st, "oy", [128, 512], BF16, 2, dma=True)
        pss = PsRing([C.ps[0], C.ps[1]], "opn")
        gcol = C.vec[:, l * V_PER + V_HGG:l * V_PER + V_HGG + 1]
        items = [(hd, g0, W) for hd in range(8) for (g0, W) in GROUPS]

        def load(it):
            hd, g0, W = it
            rows = slice(hd * 128, (hd + 1) * 128)
            of, kof, sof = ofr.next()
            a.ld(of[:, :W], Dm["oT"][0, rows, g0:g0 + W], [kof], sof)
            ob, kob, sob = obr.next()
            a.ld(ob[:, :W], Dm["oT"][1, rows, g0:g0 + W], [kob], sob)
            hg, khg, shg = hgr.next()
            a.ld(hg[:, :W], Dm["hgT"][rows, g0:g0 + W], [khg], shg)
            return of, kof, ob, kob, hg, khg

        nxt = load(items[0])
        for i, (hd, g0, W) in enumerate(items):
            of, kof, ob, kob, hg, khg = nxt
            if i + 1 < len(items):
                nxt = load(items[i + 1])
            o, ko, _ = er.next()
            a.tt("pool", o[:, :W], of[:, :W], ob[:, :W], ALU.add, [kof, kob], [ko])
            sq, ksq, _ = sqr.next()
            a.act(sq[:, :W], o[:, :W], AF.Square, [ko], [ksq])
            pn, kpn = pss.next()
            a.mm(pn[:, :W], C.onesb[:], sq[:, :W], True, True, [ksq], [kpn])
            tm, ktm, _ = er.next()
            a.act(tm[:, :W], pn[:, :W], AF.Sqrt, [kpn], [ktm], scale=1.0 / 128, bias=EPS)
            rs, krs, _ = er.next()
            a.recip(rs[:, :W], tm[:, :W], [ktm], [krs])
            y1, ky1, _ = er.next()
            a.stt(y1[:, :W], o[:, :W], gcol, rs[:, :W], ALU.mult, ALU.mult, [ko, krs], [ky1])
            y, ky, sy = yr.next()
            a.tt("pool", y[:, :W], y1[:, :W], hg[:, :W], ALU.mult, [ky1, khg], [ky])
            a.st(Dm["yhgT"][hd * 128:(hd + 1) * 128, g0:g0 + W], y[:, :W], [ky], sy)
        sch.flush()


def load_weight_bf16(C, st, name, src3, KC, NF, stage_ring):
    a = C.a
    wb = st.enter_context(C.nc.sbuf_tensor(name, [128, KC, NF], BF16))
    i = 0
    for kc0 in range(0, KC, 8):
        kn = min(8, KC - kc0)
        for f0 in range(0, NF, 512):
            wt, kw, sw = stage_ring.next()
            a.ld(wt[:, :kn, :], src3[:, kc0:kc0 + kn, f0:f0 + 512], [kw], sw)
            a.copy("pool" if i % 2 == 0 else "dve", wb[:, kc0:kc0 + kn, f0:f0 + 512], wt[:, :kn, :], [kw], [(name, kc0, f0)])
            i += 1
    return wb


def phase_merge(C, l):
    nc, sch, a, Dm = C.nc, C.sch, C.a, C.D
    with ExitStack() as st:
        ps_alloc(C, st)
        stage = sb_ring(C, st, "mstage", [128, 8, 512], F32, 2, dma=True)
        wn = ["w_br_att", "w_br_gm", "w_br_hg"]
        wbr = [load_weight_bf16(C, st, "m" + n, Dm[n][l].rearrange("(kc p) f -> p kc f", p=128), 8, 1024, stage) for n in wn]
        wkeys = [[("m" + n, 0, f0) for f0 in (0, 512)] for n in wn]
        ysrc = [fmview(Dm[n]) for n in ("yattT", "ygmT", "yhgT")]
        yr = [sb_ring(C, st, "my%d" % i, [128, 8, 512], BF16, 2, dma=True) for i in range(3)]
        ymr = sb_ring(C, st, "mym", [128, 8, 512], BF16, 2, dma=True)
        gr = sb_ring(C, st, "mg", [128, 512], F32, 6, dma=True)
        accr = sb_ring(C, st, "macc", [128, 512], F32, 4)
        psr = PsRing(C.ps[0:6], "mps")
        ymdst = fmview(Dm["ymT"])

        def load(g0, W):
            out = []
            for i in range(3):
                y, ky, sy = yr[i].next()
                a.ld(y[:, :, :W], ysrc[i][:, :, g0:g0 + W], [ky], sy)
                out.append((y, ky))
            return out

        nxt = load(*GROUPS[0])
        for gi, (g0, W) in enumerate(GROUPS):
            ys = nxt
            if gi + 1 < len(GROUPS):
                nxt = load(*GROUPS[gi + 1])
            ym, kym, sym = ymr.next()
            for fo in range(8):
                gts = []
                for br in range(3):
                    g, kg, sg = gr.next()
                    a.ld(g[:, :W], Dm["gatesT"][(br * 8 + fo) * 128:(br * 8 + fo + 1) * 128, g0:g0 + W], [kg], sg)
                    gts.append((g, kg))
                acc, kacc = None, None
                for br in range(3):
                    p, kp = psr.next()
                    for kc in range(8):
                        a.mm(p[:, :W], wbr[br][:, kc, fo * 128:(fo + 1) * 128], ys[br][0][:, kc, :W], kc == 0, kc == 7,
                             [ys[br][1]] + wkeys[br], [kp])
                    g, kg = gts[br]
                    t, kt, _ = accr.next()
                    a.tt("dve", t[:, :W], p[:, :W], g[:, :W], ALU.mult, [kp, kg], [kt])
                    if br == 0:
                        acc, kacc = t, kt
                    elif br == 1:
                        t2, kt2, _ = accr.next()
                        a.tt("pool", t2[:, :W], acc[:, :W], t[:, :W], ALU.add, [kacc, kt], [kt2])
                        acc, kacc = t2, kt2
                    else:
                        a.tt("pool", ym[:, fo, :W], acc[:, :W], t[:, :W], ALU.add, [kacc, kt], [kym])
            a.st(ymdst[:, :, g0:g0 + W], ym[:, :, :W], [kym], sym)
        sch.flush()


def phase_outproj(C, l):
    nc, sch, a, Dm = C.nc, C.sch, C.a, C.D
    with ExitStack() as st:
        ps_alloc(C, st)
        stage = sb_ring(C, st, "ostage", [128, 8, 512], F32, 2, dma=True)
        wo = load_weight_bf16(C, st, "owout", Dm["w_out"][l].rearrange("(kc p) f -> p kc f", p=128), 8, 1024, stage)
        wk = [("owout", 0, 0), ("owout", 0, 512)]
        R = norm_rings(C, st)
        R["psn"] = PsRing([C.ps[6]], "psn")
        ymr = sb_ring(C, st, "oym", [128, 8, 512], BF16, 2, dma=True)
        xr = sb_ring(C, st, "ox", [128, 8, 512], F32, 2, dma=True)
        zr = sb_ring(C, st, "oz", [128, 8, 512], F32, 2)
        h2r = sb_ring(C, st, "oh2", [128, 8, 512], BF16, 2, dma=True)
        psr = PsRing(C.ps[0:6], "ops")
        ymsrc, xv, h2dst = fmview(Dm["ymT"]), fmview(Dm["xT"]), fmview(Dm["h2T"])
        gg = C.modv[:, l, 2]

        def load(g0, W):
            ym, kym, sym = ymr.next()
            a.ld(ym[:, :, :W], ymsrc[:, :, g0:g0 + W], [kym], sym)
            x, kx, sx = xr.next()
            a.ld(x[:, :, :W], xv[:, :, g0:g0 + W], [kx], sx)
            return ym, kym, x, kx, sx

        nxt = load(*GROUPS[0])
        for gi, (g0, W) in enumerate(GROUPS):
            ym, kym, x, kx, sx = nxt
            if gi + 1 < len(GROUPS):
                nxt = load(*GROUPS[gi + 1])
            j = 1 if g0 >= TL else 0
            z, kz, _ = zr.next()
            for fo in range(8):
                p, kp = psr.next()
                for kc in range(8):
                    a.mm(p[:, :W], wo[:, kc, fo * 128:(fo + 1) * 128], ym[:, kc, :W], kc == 0, kc == 7, [kym] + wk, [kp])
                a.copy("act" if fo % 2 == 0 else "dve", z[:, fo, :W], p[:, :W], [kp], [kz])
            sq, ksq, _ = R["sq"].next()
            psn, kpsn = R["psn"].next()
            tmp, ktmp, _ = R["ntmp"].next()
            rstd, krstd, _ = R["rstd"].next()
            fm_rstd(C, z[:, :, :W], kz, 8, W, DM, psn, kpsn, sq, ksq, tmp, ktmp, rstd, krstd)
            for kc in range(8):
                tf, ktf, _ = R["tmpf"].next()
                a.stt(tf[:, :W], z[:, kc, :W], gg[:, kc, j:j + 1], rstd[:, :W], ALU.mult, ALU.mult, [kz, krstd], [ktf])
                a.tt("pool", x[:, kc, :W], x[:, kc, :W], tf[:, :W], ALU.add, [kx, ktf], [kx])
            a.st(xv[:, :, g0:g0 + W], x[:, :, :W], [kx], sx)
            h2, kh2, sh2 = h2r.next()
            norm_mod(C, R, x[:, :, :W], kx, W, g0, l, 1, dst=lambda kc, h2=h2, kh2=kh2, W=W: (h2[:, kc, :W], kh2))
            a.st(h2dst[:, :, g0:g0 + W], h2[:, :, :W], [kh2], sh2)
        sch.flush()


def acol(t):
    return t + 1 if t < TL else t + 3


def phase_ffn_up(C, l):
    nc, sch, a, Dm = C.nc, C.sch, C.a, C.D
    with ExitStack() as st:
        ps_alloc(C, st)
        s0 = sch.new_dma_sem()
        a.ld(C.hT[:], fmview(Dm["h2T"]), ["hTall"], s0)
        wsr = sb_ring(C, st, "fws", [128, 8, 128], F32, 4, dma=True)
        wbr = sb_ring(C, st, "fwb", [128, 8, 128], BF16, 4)
        accr = sb_ring(C, st, "facc", [128, NT + 4], F32, 4)
        ulr = sb_ring(C, st, "ful", [128, 16], F32, 4)
        mr = sb_ring(C, st, "fm", [128, NT + 4], BF16, 2, dma=True)
        psr = PsRing(C.ps[0:6], "fps")
        wsrc = Dm["w_up"][l].rearrange("(kc p) f -> p kc f", p=128)
        vb = l * V_PER

        def load_w(fc):
            wt, kw, sw = wsr.next()
            a.ld(wt[:], wsrc[:, :, fc * 128:(fc + 1) * 128], [kw], sw)
            wb, kb, _ = wbr.next()
            a.copy("pool", wb[:], wt[:], [kw], [kb])
            return wb, kb

        def conv_chunk(fc, wb, kb):
            acc, kacc0, _ = accr.next()
            ul, kul, _ = ulr.next()
            cw = [C.vec[:, vb + V_CW + j * 44 + fc:vb + V_CW + j * 44 + fc + 1] for j in range(3)]
            cb = C.vec[:, vb + V_CB + fc:vb + V_CB + fc + 1]
            kg = [(kacc0, gi) for gi in range(len(GROUPS))]
            for gi, (g0, W) in enumerate(GROUPS):
                p, kp = psr.next()
                for kc in range(8):
                    a.mm(p[:, :W], wb[:, kc, :], C.hT[:, kc, g0:g0 + W], kc == 0, kc == 7, [kb, "hTall"], [kp])
                c0 = acol(g0)
                prev = [kg[gi - 1]] if gi >= 1 else []
                a.act(acc[:, c0:c0 + W], p[:, :W], AF.Identity, [kp], [kg[gi]], scale=cw[1], bias=cb)
                a.copy("act", ul[:, gi:gi + 1], p[:, W - 1:W], [kp], [(kul, gi)])
                a.stt(acc[:, c0 + 1:c0 + W], p[:, 0:W - 1], cw[0], acc[:, c0 + 1:c0 + W], ALU.mult, ALU.add, [kp, kg[gi]], [kg[gi]])
                if gi >= 1 and g0 != TL:
                    a.stt(acc[:, c0:c0 + 1], ul[:, gi - 1:gi], cw[0], acc[:, c0:c0 + 1], ALU.mult, ALU.add, [(kul, gi - 1), kg[gi]], [kg[gi]])
                a.stt(acc[:, c0 - 1:c0 + W - 1], p[:, 0:W], cw[2], acc[:, c0 - 1:c0 + W - 1], ALU.mult, ALU.add, [kp, kg[gi]] + prev, [kg[gi]] + prev)
            kacc = kg
            return acc, kacc

        nxt = (load_w(0), load_w(NFF))
        for fc in range(NFF):
            (wa, ka), (wb_, kb_) = nxt
            if fc + 1 < NFF:
                nxt = (load_w(fc + 1), load_w(NFF + fc + 1))
            aa, kaa = conv_chunk(fc, wa, ka)
            ab, kab = conv_chunk(NFF + fc, wb_, kb_)
            a.act(aa[:], aa[:], AF.Silu, kaa, kaa)
            m, km, sm = mr.next()
            a.tt("pool", m[:], aa[:], ab[:], ALU.mult, kaa + kab, [km])
            a.st(Dm["mT"][fc * 128:(fc + 1) * 128, 0:TL], m[:, 1:TL + 1], [km], sm)
            a.st(Dm["mT"][fc * 128:(fc + 1) * 128, TL:NT], m[:, TL + 3:NT + 3], [km], sm)
        sch.flush()


GROUPS256 = [(i * 256, 256) for i in range(NT // 256)]


def phase_ffn_down(C, l):
    nc, sch, a, Dm = C.nc, C.sch, C.a, C.D
    last = (l == 1)
    W = 256
    with ExitStack() as st:
        ps_alloc(C, st)
        stage = sb_ring(C, st, "dstage", [128, 2, 512], F32, 2, dma=True)
        wd = st.enter_context(nc.sbuf_tensor("dwd", [128, NFF, 1024], BF16))
        wsrc = Dm["w_down"][l].rearrange("(kc p) f -> p kc f", p=128)
        i = 0
        for kc0 in range(0, NFF, 2):
            for f0 in (0, 512):
                wt, kw, sw = stage.next()
                a.ld(wt[:], wsrc[:, kc0:kc0 + 2, f0:f0 + 512], [kw], sw)
                a.copy("pool" if i % 2 == 0 else "dve", wd[:, kc0:kc0 + 2, f0:f0 + 512], wt[:], [kw], ["dwd"])
                i += 1
        R = norm_rings(C, st, W)
        R["psn"] = PsRing([C.ps[6]], "psn")
        mr = sb_ring(C, st, "dm", [128, NFF, W], BF16, 2, dma=True)
        xr = sb_ring(C, st, "dx", [128, 8, W], F32, 2, dma=True)
        zr = sb_ring(C, st, "dz", [128, 8, W], F32, 2)
        psr = PsRing(C.ps[0:4], "dps")
        pst = PsRing([(C.ps[4], C.ps[5])], "dpst")
        otr = sb_ring(C, st, "dot", [128, 1024], F32, 2, dma=True)
        msrc, xv = fmview(Dm["mT"]), fmview(Dm["xT"])
        gg = C.modv[:, l, 5]

        def load(g0):
            m, km, sm = mr.next()
            a.ld(m[:], msrc[:, :, g0:g0 + W], [km], sm)
            x, kx, sx = xr.next()
            a.ld(x[:], xv[:, :, g0:g0 + W], [kx], sx)
            return m, km, x, kx, sx

        groups = GROUPS256[:TL // 256] if last else GROUPS256
        nxt = load(groups[0][0])
        for gi, (g0, _) in enumerate(groups):
            m, km, x, kx, sx = nxt
            if gi + 1 < len(groups):
                nxt = load(groups[gi + 1][0])
            j = 1 if g0 >= TL else 0
            z, kz, _ = zr.next()
            for fo in range(8):
                p, kp = psr.next()
                for kc in range(NFF):
                    a.mm(p[:, :W], wd[:, kc, fo * 128:(fo + 1) * 128], m[:, kc, :], kc == 0, kc == NFF - 1, [km, "dwd"], [kp])
                a.copy("act" if fo % 2 == 0 else "dve", z[:, fo, :], p[:, :W], [kp], [kz])
            sq, ksq, _ = R["sq"].next()
            psn, kpsn = R["psn"].next()
            tmp, ktmp, _ = R["ntmp"].next()
            rstd, krstd, _ = R["rstd"].next()
            fm_rstd(C, z[:], kz, 8, W, DM, psn, kpsn, sq, ksq, tmp, ktmp, rstd, krstd)
            for kc in range(8):
                tf, ktf, _ = R["tmpf"].next()
                a.stt(tf[:, :W], z[:, kc, :], gg[:, kc, j:j + 1], rstd[:, :W], ALU.mult, ALU.mult, [kz, krstd], [ktf])
                a.tt("pool", x[:, kc, :], x[:, kc, :], tf[:, :W], ALU.add, [kx, ktf], [kx])
            if not last:
                a.st(xv[:, :, g0:g0 + W], x[:], [kx], sx)
                norm_mod(C, R, x[:], kx, W, g0, l + 1, 0)
            else:
                for ti in range(W // 128):
                    (pa, pb), kp = pst.next()
                    for kc in range(8):
                        pp = pa if kc < 4 else pb
                        a.tr(pp[:, (kc % 4) * 128:(kc % 4 + 1) * 128], x[:, kc, ti * 128:(ti + 1) * 128], C.ident[:], [kx], [kp])
                    ot, kot, sot = otr.next()
                    a.copy("dve", ot[:, 0:512], pa[:], [kp], [kot])
                    a.copy("act", ot[:, 512:1024], pb[:], [kp], [kot])
                    a.st(Dm["y"][g0 + ti * 128:g0 + (ti + 1) * 128, :], ot[:], [kot], sot)
        sch.flush()


IN_SHAPES = {
    "xin": ([NT, DM], F32), "cc": ([128, 8, 2], F32), "vec": ([128, 2 * V_PER], F32), "lamv": ([128, 2, 4, 64], F32),
    "w_ada": ([2, DM, 6 * DM], F32), "w_in": ([2, DM, W_EXT], F32), "ropeC": ([128, NT], F32), "ropeS": ([128, NT], F32),
    "gm_wsT": ([2, 128, 8, 128], F32), "gm_bsbc": ([2, 128, 8, 128], F32), "gm_lng": ([2, 128, 1024], F32), "gm_lnb": ([2, 128, 1024], F32),
    "w_br_att": ([2, DM, DM], F32), "w_br_gm": ([2, DM, DM], F32), "w_br_hg": ([2, DM, DM], F32), "w_out": ([2, DM, DM], F32),
    "w_up": ([2, DM, 2 * DFF], F32), "w_down": ([2, DFF, DM], F32),
    "hmask": ([2, 128, 512], F32), "hrmk": ([128, 4], F32),
}
SCRATCH = {
    "xT": ([DM, NT], F32), "qT": ([DM, NT], BF16), "kT": ([DM, NT], BF16), "Vt": ([NT, DM], BF16), "guT": ([DM, NT], F32),
    "gvg": ([NT, DM], F32), "hqT": ([DM, NT], F32), "lfT": ([2, DM, NT], F32), "kgT": ([2, DM, NT], F32), "hv": ([NT, DM], BF16),
    "hgT": ([DM, NT], F32), "gatesT": ([3 * DM, NT], F32), "yattT": ([DM, NT], BF16), "ygmT": ([DM, NT], BF16), "yhgT": ([DM, NT], BF16),
    "qtil": ([2, DM, NT], BF16), "ktilT": ([2, DM, NT], BF16), "ktok": ([2, 8, NT, 128], BF16), "hsc": ([2, 8, 2, 128, 4, NCS], F32),
    "oT": ([2, DM, NT], F32), "ymT": ([DM, NT], BF16), "h2T": ([DM, NT], BF16), "mT": ([DFF, NT], BF16),
}
PHASES = ["prologue", "inproj0", "attn0", "gmlp0", "hprep0", "hscan0", "hout0", "merge0", "outproj0", "ffnup0", "ffndown0",
          "inproj1", "attn1", "gmlp1", "hprep1", "hscan1", "hout1", "merge1", "outproj1", "ffnup1", "ffndown1"]


def build_nc(stop_after=None, dbg=()):
    nc = bass.Bass("TRN2", target_bir_lowering=False)
    Dm = {}
    for n, (shp, dt) in IN_SHAPES.items():
        Dm[n] = nc.dram_tensor("i_" + n, shp, dt, kind="ExternalInput").ap()
    for n, (shp, dt) in SCRATCH.items():
        Dm[n] = nc.dram_tensor("s_" + n, shp, dt, kind="ExternalOutput" if n in dbg else "Internal").ap()
    Dm["y"] = nc.dram_tensor("y", [TL, DM], F32, kind="ExternalOutput").ap()
    with ExitStack() as st0:
        C = Ctx()
        C.nc, C.D = NCProxy(nc), Dm
        C.sch = Sched(nc, st0, n_dma_sems=48)
        C.a = A(C.sch)
        sb = lambda n, shp, dt=F32: st0.enter_context(C.nc.sbuf_tensor(n, shp, dt))
        C.ident, C.identb, C.onesb = sb("ident", [128, 128]), sb("identb", [128, 128], BF16), sb("onesb", [128, 128], BF16)
        C.vec, C.modv = sb("vec", [128, 2 * V_PER]), sb("modv", [128, 2, 6, 8, 2])
        C.lb, C.oml = sb("lb", [128, 2, 2, 8]), sb("oml", [128, 2, 2, 8])
        C.neglam, C.attg = sb("neglam", [128, 2]), sb("attg", [128, 2])
        C.maskf, C.maskb = sb("maskf", [64, 32]), sb("maskb", [64, 32])
        plan = [
            ("hT+",), ("prologue", phase_prologue), ("inproj0", phase_inproj, 0), ("hT-",),
            ("attn0", phase_attn, 0), ("gmlp0", phase_gmlp, 0), ("hprep0", phase_hgrn_prep, 0), ("hscan0", phase_hgrn_scan, 0),
            ("hout0", phase_hgrn_out, 0), ("merge0", phase_merge, 0), ("outproj0", phase_outproj, 0),
            ("hT+",), ("ffnup0", phase_ffn_up, 0), ("ffndown0", phase_ffn_down, 0), ("inproj1", phase_inproj, 1), ("hT-",),
            ("attn1", phase_attn, 1), ("gmlp1", phase_gmlp, 1), ("hprep1", phase_hgrn_prep, 1), ("hscan1", phase_hgrn_scan, 1),
            ("hout1", phase_hgrn_out, 1), ("merge1", phase_merge, 1), ("outproj1", phase_outproj, 1),
            ("hT+",), ("ffnup1", phase_ffn_up, 1), ("ffndown1", phase_ffn_down, 1), ("hT-",),
        ]
        hst = None
        for item in plan:
            if item[0] == "hT+":
                hst = ExitStack()
                C.hT = hst.enter_context(C.nc.sbuf_tensor("hT", [128, 8, NT], BF16))
                continue
            if item[0] == "hT-":
                hst.close()
                hst = None
                continue
            item[1](C, *item[2:])
            if stop_after == item[0]:
                break
        if hst is not None:
            hst.close()
    return nc


def _col(v):
    return np.ascontiguousarray(v.reshape(-1, 128).T)


def prep_shared(inp):
    f32 = np.float32
    sh = {}
    sh["w_ada"] = np.ascontiguousarray(inp["w_ada"], dtype=f32)
    w_in = inp["w_in"]
    offs = np.cumsum([0, 1024, 1024, 1024, 1024, 1024, 1024, 1024, 1024, 1024, 1024, 3072])
    aq, ak, av, gu, gv, hq, hff, hfb, hi, hg, gates = [w_in[:, :, offs[i]:offs[i + 1]] for i in range(11)]
    perm64 = np.concatenate([np.arange(16, 32), np.arange(0, 16), np.arange(48, 64), np.arange(32, 48)])
    perm128 = np.concatenate([perm64, 64 + perm64])
    cols = []
    for src in (aq, ak):
        for hd in range(8):
            blk = src[:, :, hd * 128:(hd + 1) * 128]
            cols += [blk, blk[:, :, perm128]]
    cols += [gu, hq, hff, hfb, hg, gates, av, gv, hi]
    sh["w_in"] = np.ascontiguousarray(np.concatenate(cols, axis=2), dtype=f32)
    assert sh["w_in"].shape[2] == W_EXT
    half = 16
    inv = (10000.0 ** (-np.arange(half, dtype=np.float32) / half)).astype(f32)
    t = np.arange(TL)
    rows, colsp = (t // 64).astype(f32), (t % 64).astype(f32)
    Cq = np.ones((64, NT), f32)
    Sq = np.zeros((64, NT), f32)
    for base, pos in ((0, rows), (32, colsp)):
        ang = pos[None, :] * inv[:, None]
        c, s_ = np.cos(ang).astype(f32), np.sin(ang).astype(f32)
        Cq[base:base + 16, :TL] = c
        Cq[base + 16:base + 32, :TL] = c
        Sq[base:base + 16, :TL] = -s_
        Sq[base + 16:base + 32, :TL] = s_
    sh["ropeC"] = np.ascontiguousarray(np.concatenate([Cq, Cq], axis=0))
    sh["ropeS"] = np.ascontiguousarray(np.concatenate([Sq, Sq], axis=0))
    si, ti = np.arange(128)[:, None], np.arange(128)[None, :]
    same = (si // HC) == (ti // HC)
    mf = (same & (si <= ti)).astype(f32)
    mb = (same & (si >= ti)).astype(f32)
    sh["hmask"] = np.ascontiguousarray(np.stack([np.tile(mf, (1, 4)), np.tile(mb, (1, 4))]))
    sh["hrmk"] = np.ascontiguousarray((np.arange(128)[:, None] // HC == np.arange(4)[None, :]).astype(f32))
    vec = np.zeros((128, 2 * V_PER), f32)
    for l in range(2):
        b = l * V_PER
        vec[:, b + V_BADA:b + V_BADA + 48] = _col(inp["b_ada"][l])
        vec[:, b + V_GPRE1:b + V_GPRE1 + 8] = _col(inp["g_pre_mix"][l])
        vec[:, b + V_GPOST1:b + V_GPOST1 + 8] = _col(inp["g_post_mix"][l])
        vec[:, b + V_GPRE2:b + V_GPRE2 + 8] = _col(inp["g_pre_ffn"][l])
        vec[:, b + V_GPOST2:b + V_GPOST2 + 8] = _col(inp["g_post_ffn"][l])
        for j in range(3):
            vec[:, b + V_CW + j * 44:b + V_CW + (j + 1) * 44] = _col(inp["conv_w"][l, j])
        vec[:, b + V_CB:b + V_CB + 44] = _col(inp["conv_b"][l])
        vec[:, b + V_ATTG] = inp["att_subln_g"][l]
        vec[:, b + V_HGG] = inp["hg_norm_g"][l]
    for l2 in range(2):
        for d in range(2):
            vec[:, V_LB + l2 * 16 + d * 8:V_LB + l2 * 16 + d * 8 + 8] = _col(inp["hg_lb"][l2, d])
    sh["vec"] = vec
    lamv = np.stack([np.stack([inp[n][l] for n in ("lam_q1", "lam_k1", "lam_q2", "lam_k2")]) for l in range(2)])
    sh["lamv"] = np.ascontiguousarray(np.broadcast_to(lamv[None], (128, 2, 4, 64)), dtype=f32)
    sh["gm_wsT"] = np.ascontiguousarray(np.transpose(inp["gm_ws"], (0, 3, 1, 2)), dtype=f32)
    sh["gm_bsbc"] = np.ascontiguousarray(np.broadcast_to(inp["gm_bs"][:, None], (2, 128, 8, 128)), dtype=f32)
    sh["gm_lng"] = np.ascontiguousarray(np.broadcast_to(inp["gm_ln_g"][:, None], (2, 128, 1024)), dtype=f32)
    sh["gm_lnb"] = np.ascontiguousarray(np.broadcast_to(inp["gm_ln_b"][:, None], (2, 128, 1024)), dtype=f32)
    for n in ("w_br_att", "w_br_gm", "w_br_hg", "w_out", "w_up", "w_down"):
        sh[n] = np.ascontiguousarray(inp[n], dtype=f32)
    return sh


def prep_core(inp, b):
    f32 = np.float32
    d = {}
    d["xin"] = np.ascontiguousarray(np.concatenate([inp["x"][b], inp["ctx"][b]], axis=0), dtype=f32)
    cc = np.stack([_col(inp["c"][b]), _col(inp["c_ctx"])], axis=-1)
    d["cc"] = np.ascontiguousarray(cc, dtype=f32)
    return d


def kernel(**inputs):
    inp = {k: np.asarray(v) for k, v in inputs.items()}
    nc = build_nc()
    sh = prep_shared(inp)
    in_maps = []
    for b in range(8):
        m = dict(sh)
        m.update(prep_core(inp, b))
        in_maps.append({"i_" + k: v for k, v in m.items()})
    res = run_bass_kernel_spmd(nc, in_maps, core_ids=list(range(8)))
    return np.stack([np.asarray(r["y"], dtype=np.float32) for r in res.results], axis=0)
```

```python
import math
from contextlib import ExitStack
import numpy as np
import concourse.bass as bass
import concourse.mybir as mybir
from concourse.bass_utils import run_bass_kernel_spmd

F32 = mybir.dt.float32
BF16 = mybir.dt.bfloat16
AF = mybir.ActivationFunctionType
ALU = mybir.AluOpType
AX = mybir.AxisListType

ENGS = ("pe", "act", "dve", "pool", "sp")


class Sched:
    def __init__(self, nc, stack, n_dma_sems=40):
        self.nc = nc
        self.esem = {e: stack.enter_context(nc.semaphore("s_" + e)) for e in ENGS}
        self.ecnt = {e: 0 for e in ENGS}
        self.dsem = [stack.enter_context(nc.semaphore("d%d" % i)) for i in range(n_dma_sems)]
        self.dcnt = [0] * n_dma_sems
        self.dsem_next = 0
        self.reset()

    def reset(self):
        self.ops = []
        self.lastw = {}
        self.readers = {}

    def new_dma_sem(self):
        i = self.dsem_next
        self.dsem_next += 1
        assert i < len(self.dsem), "out of DMA semaphores"
        return i

    def _add(self, eng, fn, reads, writes, dsem):
        i = len(self.ops)
        deps = set()
        for k in reads:
            w = self.lastw.get(k)
            if w is not None:
                deps.add(w)
        for k in writes:
            w = self.lastw.get(k)
            if w is not None:
                deps.add(w)
            for r in self.readers.get(k, ()):
                deps.add(r)
        deps.discard(i)
        for k in reads:
            self.readers.setdefault(k, []).append(i)
        for k in writes:
            self.lastw[k] = i
            self.readers[k] = []
        self.ops.append([eng, fn, deps, dsem, None])
        return i

    def op(self, eng, fn, reads=(), writes=()):
        return self._add(eng, fn, reads, writes, None)

    def dma(self, eng, fn, reads=(), writes=(), sem=None):
        assert sem is not None
        return self._add(eng, fn, reads, writes, sem)

    def flush(self):
        nc = self.nc
        ops = self.ops
        needed = set()
        for o in ops:
            eng, fn, deps, dsem, _ = o
            for d in deps:
                od = ops[d]
                if od[3] is None and od[0] == "pe" and eng == "pe" and dsem is None:
                    continue
                needed.add(d)
        last = {}
        for i, o in enumerate(ops):
            if o[3] is None:
                last[o[0]] = i
        for e, i in last.items():
            needed.add(i)
        for i, o in enumerate(ops):
            eng, fn, deps, dsem, _ = o
            if dsem is not None:
                self.dcnt[dsem] += 16
                o[4] = (("d", dsem), self.dcnt[dsem])
            elif i in needed:
                self.ecnt[eng] += 1
                o[4] = (("e", eng), self.ecnt[eng])
        final = {}
        for o in ops:
            if o[4] is not None:
                final[o[4][0]] = max(final.get(o[4][0], 0), o[4][1])
        per_eng = {e: [] for e in ENGS}
        for i, o in enumerate(ops):
            per_eng[o[0]].append(i)

        def semobj(sid):
            return self.esem[sid[1]] if sid[0] == "e" else self.dsem[sid[1]]

        def emit(eng_name, eng):
            waited = {}
            for i in per_eng[eng_name]:
                _, fn, deps, dsem, sig = ops[i]
                w = {}
                for d in deps:
                    s = ops[d][4]
                    if s is None:
                        continue
                    if eng_name == "pe" and ops[d][0] == "pe" and ops[d][3] is None and dsem is None:
                        continue
                    if s[1] > w.get(s[0], 0):
                        w[s[0]] = s[1]
                for sid, val in w.items():
                    if waited.get(sid, 0) >= val:
                        continue
                    waited[sid] = val
                    eng.wait_ge(semobj(sid), val)
                ins = fn(eng)
                if sig is not None:
                    ins.then_inc(semobj(sig[0]), 16 if sig[0][0] == "d" else 1)
            for sid, val in final.items():
                if waited.get(sid, 0) >= val:
                    continue
                eng.wait_ge(semobj(sid), val)

        with nc.Block() as block:
            @block.tensor
            def _(e):
                emit("pe", e)

            @block.scalar
            def _(e):
                emit("act", e)

            @block.vector
            def _(e):
                emit("dve", e)

            @block.gpsimd
            def _(e):
                emit("pool", e)

            @block.sync
            def _(e):
                emit("sp", e)
        n = len(ops)
        self.reset()
        self.dsem_next = 0
        return n


class Ring:
    def __init__(self, sch, tiles, name, dma=False):
        self.tiles = tiles
        self.keys = [(name, i) for i in range(len(tiles))]
        self.sems = [sch.new_dma_sem() for _ in tiles] if dma else [None] * len(tiles)
        self.i = -1

    def next(self):
        self.i = (self.i + 1) % len(self.tiles)
        return self.tiles[self.i], self.keys[self.i], self.sems[self.i]


NT, TL, CL, DM = 4352, 4096, 256, 1024
GROUPS = [(i * 512, 512) for i in range(8)] + [(4096, 256)]
EPS = 1e-6
DFF = 2816
NFF = DFF // 128
HC = 32
NCH = NT // HC
V_BADA, V_GPRE1, V_GPOST1, V_GPRE2, V_GPOST2, V_CW, V_CB, V_ATTG, V_HGG, V_LB = 0, 48, 56, 64, 72, 80, 212, 256, 257, 258
V_PER = 258 + 32 + 2
FC_Q, FC_K, FC_GU, FC_HQ, FC_HF, FC_HB, FC_HG, FC_GATE = 0, 16, 32, 40, 48, 56, 64, 72
N_FM = 96
TM_V, TM_GV, TM_HI = 0, 2, 4
W_EXT = N_FM * 128 + 6 * 512


class A:
    def __init__(self, sch):
        self.s = sch

    def act(self, out, in_, func, r, w, **kw):
        self.s.op("act", lambda e: e.activation(out=out, in_=in_, func=func, **kw), r, w)

    def tt(self, eng, out, in0, in1, op, r, w):
        self.s.op(eng, lambda e: e.tensor_tensor(out=out, in0=in0, in1=in1, op=op), r, w)

    def ts(self, eng, out, in0, s1, s2, op0, op1, r, w):
        self.s.op(eng, lambda e: e.tensor_scalar(out=out, in0=in0, scalar1=s1, scalar2=s2, op0=op0, op1=op1), r, w)

    def stt(self, out, in0, scalar, in1, op0, op1, r, w):
        self.s.op("dve", lambda e: e.scalar_tensor_tensor(out=out, in0=in0, scalar=scalar, in1=in1, op0=op0, op1=op1), r, w)

    def copy(self, eng, out, in_, r, w):
        if eng == "act":
            self.s.op("act", lambda e: e.copy(out=out, in_=in_), r, w)
        else:
            self.s.op(eng, lambda e: e.tensor_copy(out=out, in_=in_), r, w)

    def recip(self, out, in_, r, w):
        self.s.op("dve", lambda e: e.reciprocal(out=out, in_=in_), r, w)

    def memset(self, eng, ap, val, w):
        self.s.op(eng, lambda e: e.memset(ap, val), (), w)

    def mm(self, out, lhsT, rhs, start, stop, r, w):
        self.s.op("pe", lambda e: e.matmul(out, lhsT=lhsT, rhs=rhs, start=start, stop=stop), r, w)

    def tr(self, out, in_, ident, r, w):
        self.s.op("pe", lambda e: e.transpose(out, in_, ident), r, w)

    def ld(self, out, in_, w, sem, r=(), q="sp"):
        self.s.dma(q, lambda e: e.dma_start(out=out, in_=in_), r, w, sem)

    def st(self, out, in_, r, sem, w=(), q="sp"):
        self.s.dma(q, lambda e: e.dma_start(out=out, in_=in_), r, w, sem)


class Ctx:
    pass


class NCProxy:
    def __init__(self, nc):
        self._nc = nc
        self._n = 0

    def __getattr__(self, k):
        return getattr(self._nc, k)

    def sbuf_tensor(self, name, shape, dt):
        self._n += 1
        return self._nc.sbuf_tensor("%s_%d" % (name, self._n), shape, dt)

    def psum_tensor(self, name, shape, dt):
        self._n += 1
        return self._nc.psum_tensor("%s_%d" % (name, self._n), shape, dt)


def ps_alloc(C, st, with_bf16=False):
    n = 7 if with_bf16 else 8
    C.ps = [st.enter_context(C.nc.psum_tensor("ps%d" % i, [128, 512], F32)) for i in range(n)]
    C.psb = st.enter_context(C.nc.psum_tensor("psb", [128, 1024], BF16)) if with_bf16 else None


def sb_ring(C, st, name, shape, dt, n, dma=False):
    tiles = [st.enter_context(C.nc.sbuf_tensor("%s%d" % (name, i), shape, dt)) for i in range(n)]
    return Ring(C.sch, tiles, name, dma)


class PsRing:
    def __init__(self, tiles, name):
        self.tiles = tiles
        self.keys = [(name, i) for i in range(len(tiles))]
        self.i = -1

    def next(self):
        self.i = (self.i + 1) % len(self.tiles)
        return self.tiles[self.i], self.keys[self.i]


def fm_rstd(C, xt, kx, KC, W, nfeat, psn, kpsn, sq, ksq, tmp, ktmp, rstd, krstd):
    a = C.a
    a.act(sq[:, :KC, :W], xt, AF.Square, [kx], [ksq])
    for kc in range(KC):
        a.mm(psn[:, :W], C.onesb[:], sq[:, kc, :W], kc == 0, kc == KC - 1, [ksq], [kpsn])
    a.act(tmp[:, :W], psn[:, :W], AF.Sqrt, [kpsn], [ktmp], scale=1.0 / nfeat, bias=EPS)
    a.recip(rstd[:, :W], tmp[:, :W], [ktmp], [krstd])


def norm_mod(C, R, xt, kx, W, g0, l, which, dst=None):
    a = C.a
    j = 1 if g0 >= TL else 0
    gm = C.modv[:, l, 1 + 3 * which]
    sh = C.modv[:, l, 0 + 3 * which]
    sq, ksq, _ = R["sq"].next()
    psn, kpsn = R["psn"].next()
    tmp, ktmp, _ = R["ntmp"].next()
    rstd, krstd, _ = R["rstd"].next()
    fm_rstd(C, xt, kx, 8, W, DM, psn, kpsn, sq, ksq, tmp, ktmp, rstd, krstd)
    for kc in range(8):
        tf, ktf, _ = R["tmpf"].next()
        a.stt(tf[:, :W], xt[:, kc, :], gm[:, kc, j:j + 1], rstd[:, :W], ALU.mult, ALU.mult, [kx, krstd], [ktf])
        if dst is None:
            o_ap, o_key = C.hT[:, kc, g0:g0 + W], ("hT", kc, g0)
        else:
            o_ap, o_key = dst(kc)
        a.act(o_ap, tf[:, :W], AF.Identity, [ktf], [o_key], bias=sh[:, kc, j:j + 1], scale=1.0)


def norm_rings(C, st, W=512):
    R = {}
    R["sq"] = sb_ring(C, st, "nsq", [128, 8, W], BF16, 2)
    R["ntmp"] = sb_ring(C, st, "ntmp", [128, W], F32, 2)
    R["rstd"] = sb_ring(C, st, "nrstd", [128, W], F32, 2)
    R["tmpf"] = sb_ring(C, st, "ntf", [128, W], F32, 3)
    return R


def phase_prologue(C):
    nc, sch, a, Dm = C.nc, C.sch, C.a, C.D
    with ExitStack() as st:
        ps_alloc(C, st)
        sems = [sch.new_dma_sem() for _ in range(4)]
        a.memset("pool", C.ident[:], 1.0, ["ident"])
        sch.op("pool", lambda e: e.affine_select(out=C.ident[:], in_=C.ident[:], pattern=[[-1, 128]], compare_op=ALU.is_equal,
                                                 fill=0.0, base=0, channel_multiplier=1), ["ident"], ["ident"])
        a.copy("pool", C.identb[:], C.ident[:], ["ident"], ["identb"])
        a.memset("pool", C.onesb[:], 1.0, ["onesb"])
        for nm, sg_ in (("maskf", 1), ("maskb", -1)):
            m = getattr(C, nm)
            a.memset("pool", m[0:32, :], 1.0, [nm])
            sch.op("pool", lambda e, m=m, sg_=sg_: e.affine_select(
                out=m[0:32, :], in_=m[0:32, :], pattern=[[sg_, 32]], compare_op=ALU.is_ge,
                fill=0.0, base=0, channel_multiplier=-sg_), [nm], [nm])
            a.ld(m[32:64, :], m[0:32, :], [(nm, "hi")], sch.new_dma_sem(), r=[nm])
        a.ld(C.vec[:], Dm["vec"], ["vec"], sems[3])
        lamt = st.enter_context(nc.sbuf_tensor("lamt", [128, 2, 4, 64], F32))
        a.ld(lamt[:], Dm["lamv"], ["lamt"], sems[2])
        cc = st.enter_context(nc.sbuf_tensor("cc", [128, 8, 2], F32))
        a.ld(cc[:], Dm["cc"], ["cc"], sems[0])
        scc = st.enter_context(nc.sbuf_tensor("scc", [128, 8, 2], F32))
        a.act(scc[:], cc[:], AF.Silu, ["cc"], ["scc"])
        lt = st.enter_context(nc.sbuf_tensor("lt", [128, 2, 2, 64], F32))
        ls = st.enter_context(nc.sbuf_tensor("ls", [128, 4], F32))
        le = st.enter_context(nc.sbuf_tensor("le", [128, 4], F32))
        for l in range(2):
            for i in range(2):
                a.tt("dve", lt[:, l, i, :], lamt[:, l, 2 * i, :], lamt[:, l, 2 * i + 1, :], ALU.mult, ["lamt"], ["lt"])
                sch.op("dve", lambda e, l=l, i=i: e.reduce_sum(out=ls[:, 2 * l + i:2 * l + i + 1], in_=lt[:, l, i, :], axis=AX.X), ["lt"], ["ls"])
        a.act(le[:], ls[:], AF.Exp, ["ls"], ["le"])
        for l in range(2):
            lam_init = 0.8 - 0.6 * math.exp(-0.3 * l)
            a.stt(C.neglam[:, l:l + 1], le[:, 2 * l + 1:2 * l + 2], -lam_init, le[:, 2 * l:2 * l + 1], ALU.add, ALU.subtract, ["le"], ["neglam"])
            a.ts("dve", C.attg[:, l:l + 1], C.vec[:, l * V_PER + V_ATTG:l * V_PER + V_ATTG + 1], 1.0 - lam_init, None, ALU.mult, ALU.bypass,
                 ["vec"], ["attg"])
        a.memset("pool", C.lb[:, 0], 0.0, ["lb"])
        a.memset("pool", C.oml[:, 0], 1.0, ["oml"])
        lbd = st.enter_context(nc.sbuf_tensor("lbd", [128, 16], F32))
        a.tt("dve", lbd[:], C.vec[:, V_LB + 16:V_LB + 32], C.vec[:, V_LB:V_LB + 16], ALU.subtract, ["vec"], ["lbd"])
        a.act(C.lb[:, 1].rearrange("p d h -> p (d h)"), lbd[:], AF.Sigmoid, ["lbd"], ["lb"])
        a.ts("dve", C.oml[:, 1].rearrange("p d h -> p (d h)"), C.lb[:, 1].rearrange("p d h -> p (d h)"), -1.0, 1.0, ALU.mult, ALU.add, ["lb"], ["oml"])
        wring = sb_ring(C, st, "wada", [128, 8, 512], F32, 2, dma=True)
        modr = st.enter_context(nc.sbuf_tensor("modr", [128, 2, 48, 2], F32))
        psm = C.ps[0]
        for l in range(2):
            for fb in range(12):
                wt, kw, sw = wring.next()
                a.ld(wt[:], Dm["w_ada"][l].rearrange("(kc p) f -> p kc f", p=128)[:, :, fb * 512:(fb + 1) * 512], [kw], sw)
                for fi in range(4):
                    f = fb * 4 + fi
                    for kc in range(8):
                        a.mm(psm[:, 2 * f:2 * f + 2], wt[:, kc, fi * 128:(fi + 1) * 128], scc[:, kc, :], kc == 0, kc == 7, [kw, "scc"], ["psm"])
            for j in range(2):
                a.tt("dve", modr[:, l, :, j], psm[:, 0:96].rearrange("p (f j) -> p f j", j=2)[:, :, j],
                     C.vec[:, l * V_PER + V_BADA:l * V_PER + V_BADA + 48], ALU.add, ["psm", "vec"], ["modr"])
            for w2 in range(2):
                base = 24 * w2
                gpre = C.vec[:, l * V_PER + (V_GPRE1 if w2 == 0 else V_GPRE2):][:, 0:8]
                gpost = C.vec[:, l * V_PER + (V_GPOST1 if w2 == 0 else V_GPOST2):][:, 0:8]
                for j in range(2):
                    a.copy("dve", C.modv[:, l, 3 * w2 + 0, :, j], modr[:, l, base:base + 8, j], ["modr"], ["modv"])
                    a.stt(C.modv[:, l, 3 * w2 + 1, :, j], modr[:, l, base + 8:base + 16, j], 1.0, gpre, ALU.add, ALU.mult, ["modr", "vec"], ["modv"])
                    a.tt("dve", C.modv[:, l, 3 * w2 + 2, :, j], modr[:, l, base + 16:base + 24, j], gpost, ALU.mult, ["modr", "vec"], ["modv"])
        R = norm_rings(C, st)
        R["psn"] = PsRing([C.ps[1]], "psn")
        xin = sb_ring(C, st, "xin", [128, 1024], F32, 2, dma=True)
        xg = sb_ring(C, st, "xg", [128, 8, 512], F32, 2, dma=True)
        pst = PsRing([(C.ps[2], C.ps[3]), (C.ps[4], C.ps[5])], "pst")
        for (g0, W) in GROUPS:
            xgt, kxg, sxg = xg.next()
            for ti in range(W // 128):
                xt, kxt, sxt = xin.next()
                a.ld(xt[:], Dm["xin"][g0 + ti * 128:g0 + (ti + 1) * 128, :], [kxt], sxt)
                (pa, pb), kp = pst.next()
                for kc in range(8):
                    pp = pa if kc < 4 else pb
                    a.tr(pp[:, (kc % 4) * 128:(kc % 4 + 1) * 128], xt[:, kc * 128:(kc + 1) * 128], C.ident[:], [kxt, "ident"], [kp])
                a.copy("dve", xgt[:, 0:4, ti * 128:(ti + 1) * 128], pa[:].rearrange("p (k t) -> p k t", t=128), [kp], [kxg])
                a.copy("act", xgt[:, 4:8, ti * 128:(ti + 1) * 128], pb[:].rearrange("p (k t) -> p k t", t=128), [kp], [kxg])
            a.st(Dm["xT"].rearrange("(kc p) t -> p kc t", p=128)[:, :, g0:g0 + W], xgt[:, :, :W], [kxg], sxg)
            norm_mod(C, R, xgt[:, :, :W], kxg, W, g0, 0, 0)
        sch.flush()


def fmview(ap):
    return ap.rearrange("(fc p) t -> p fc t", p=128)


def phase_inproj(C, l):
    nc, sch, a, Dm = C.nc, C.sch, C.a, C.D
    with ExitStack() as st:
        ps_alloc(C, st)
        ropeC = st.enter_context(nc.sbuf_tensor("ropeC", [128, NT], F32))
        ropeS = st.enter_context(nc.sbuf_tensor("ropeS", [128, NT], F32))
        s0 = sch.new_dma_sem()
        a.ld(ropeC[:], Dm["ropeC"], ["ropeC"], s0)
        a.ld(ropeS[:], Dm["ropeS"], ["ropeS"], sch.new_dma_sem())
        wst = sb_ring(C, st, "wst", [128, 8, 512], F32, 2, dma=True)
        wbf = sb_ring(C, st, "wbf", [128, 8, 512], BF16, 2)
        o32 = sb_ring(C, st, "o32", [128, 512], F32, 6, dma=True)
        o16 = sb_ring(C, st, "o16", [128, 512], BF16, 4, dma=True)
        t32 = sb_ring(C, st, "t32", [128, 512], F32, 6)
        psr = PsRing(C.ps[0:6], "psA")
        wsrc = Dm["w_in"][l].rearrange("(kc p) f -> p kc f", p=128)
        nblk = W_EXT // 512
        lbv, omlv = C.lb, C.oml

        def load_w(b):
            wt, kw, sw = wst.next()
            a.ld(wt[:], wsrc[:, :, b * 512:(b + 1) * 512], [kw], sw)
            wb, kb, _ = wbf.next()
            a.copy("pool", wb[:], wt[:], [kw], [kb])
            return wb, kb

        def fm_mm(wb, kb, fi, g0, W):
            p, kp = psr.next()
            for kc in range(8):
                a.mm(p[:, :W], wb[:, kc, fi * 128:(fi + 1) * 128], C.hT[:, kc, g0:g0 + W], kc == 0, kc == 7, [kb], [kp])
            return p, kp

        nxt = load_w(0)
        for b in range(nblk):
            wb, kb = nxt
            if b + 1 < nblk:
                nxt = load_w(b + 1)
            if b < N_FM // 4:
                for (g0, W) in GROUPS:
                    fcs = [b * 4 + i for i in range(4)]
                    if fcs[0] < FC_GU:
                        for pi in range(2):
                            fc = fcs[2 * pi]
                            p1, kp1 = fm_mm(wb, kb, 2 * pi, g0, W)
                            p2, kp2 = fm_mm(wb, kb, 2 * pi + 1, g0, W)
                            t1, kt1, _ = t32.next()
                            t2, kt2, _ = t32.next()
                            a.tt("dve", t1[:, :W], p1[:, :W], ropeC[:, g0:g0 + W], ALU.mult, [kp1, "ropeC"], [kt1])
                            a.tt("dve", t2[:, :W], p2[:, :W], ropeS[:, g0:g0 + W], ALU.mult, [kp2, "ropeS"], [kt2])
                            o, ko, so = o16.next()
                            a.tt("pool", o[:, :W], t1[:, :W], t2[:, :W], ALU.add, [kt1, kt2], [ko])
                            isk = fc >= FC_K
                            hd = ((fc - FC_K) if isk else fc) // 2
                            dst = Dm["kT" if isk else "qT"]
                            a.st(dst[hd * 128:(hd + 1) * 128, g0:g0 + W], o[:, :W], [ko], so)
                        continue
                    for fi, fc in enumerate(fcs):
                        p, kp = fm_mm(wb, kb, fi, g0, W)
                        if fc < FC_HQ:
                            o, ko, so = o32.next()
                            a.act(o[:, :W], p[:, :W], AF.Gelu_apprx_tanh, [kp], [ko])
                            a.st(Dm["guT"][(fc - FC_GU) * 128:(fc - FC_GU + 1) * 128, g0:g0 + W], o[:, :W], [ko], so)
                        elif fc < FC_HF:
                            o, ko, so = o32.next()
                            a.act(o[:, :W], p[:, :W], AF.Silu, [kp], [ko])
                            a.st(Dm["hqT"][(fc - FC_HQ) * 128:(fc - FC_HQ + 1) * 128, g0:g0 + W], o[:, :W], [ko], so)
                        elif fc < FC_HG:
                            d = 0 if fc < FC_HB else 1
                            hd = fc - (FC_HF if d == 0 else FC_HB)
                            t1, kt1, _ = t32.next()
                            a.act(t1[:, :W], p[:, :W], AF.Sigmoid, [kp], [kt1])
                            t2, kt2, _ = t32.next()
                            a.ts("dve", t2[:, :W], t1[:, :W], omlv[:, l, d, hd:hd + 1], lbv[:, l, d, hd:hd + 1], ALU.mult, ALU.add, [kt1], [kt2])
                            o, ko, so = o32.next()
                            a.act(o[:, :W], t2[:, :W], AF.Ln, [kt2], [ko])
                            a.st(Dm["lfT"][d, hd * 128:(hd + 1) * 128, g0:g0 + W], o[:, :W], [ko], so)
                            o2, ko2, so2 = o32.next()
                            a.ts("pool", o2[:, :W], t2[:, :W], -1.0, 1.0, ALU.mult, ALU.add, [kt2], [ko2])
                            a.st(Dm["kgT"][d, hd * 128:(hd + 1) * 128, g0:g0 + W], o2[:, :W], [ko2], so2)
                        elif fc < FC_GATE:
                            o, ko, so = o32.next()
                            a.act(o[:, :W], p[:, :W], AF.Silu, [kp], [ko])
                            a.st(Dm["hgT"][(fc - FC_HG) * 128:(fc - FC_HG + 1) * 128, g0:g0 + W], o[:, :W], [ko], so)
                        else:
                            o, ko, so = o32.next()
                            a.act(o[:, :W], p[:, :W], AF.Sigmoid, [kp], [ko])
                            a.st(Dm["gatesT"][(fc - FC_GATE) * 128:(fc - FC_GATE + 1) * 128, g0:g0 + W], o[:, :W], [ko], so)
            else:
                tb = b - N_FM // 4
                fam, half = tb // 2, tb % 2
                for tt_ in range(NT // 128):
                    p, kp = psr.next()
                    for kc in range(8):
                        a.mm(p[:, :], C.hT[:, kc, tt_ * 128:(tt_ + 1) * 128], wb[:, kc, :], kc == 0, kc == 7, [kb], [kp])
                    rows = slice(tt_ * 128, (tt_ + 1) * 128)
                    cols = slice(half * 512, (half + 1) * 512)
                    if fam == 1:
                        o, ko, so = o32.next()
                        a.act(o[:, :], p[:, :], AF.Gelu_apprx_tanh, [kp], [ko])
                        a.st(Dm["gvg"][rows, cols], o[:, :], [ko], so)
                    else:
                        o, ko, so = o16.next()
                        a.copy("dve" if tt_ % 2 == 0 else "act", o[:, :], p[:, :], [kp], [ko])
                        a.st(Dm["Vt" if fam == 0 else "hv"][rows, cols], o[:, :], [ko], so)
        sch.flush()


def phase_attn(C, l):
    nc, sch, a, Dm = C.nc, C.sch, C.a, C.D
    last = (l == 1)
    with ExitStack() as st:
        ps_alloc(C, st)
        kTr = sb_ring(C, st, "akT", [128, NT], BF16, 2, dma=True)
        qTr = [sb_ring(C, st, "aqT%d" % i, [128, NT], BF16, 2, dma=True) for i in range(2)]
        for i in range(2):
            for t_, k_ in zip(qTr[i].tiles, qTr[i].keys):
                a.memset("pool", t_[(1 - i) * 64:(2 - i) * 64, :], 0.0, [k_])
        Vr = sb_ring(C, st, "aV", [128, 34, 128], BF16, 2, dma=True)
        pTr = sb_ring(C, st, "apT", [128, 512], BF16, 6)
        er = sb_ring(C, st, "ae", [128, 512], F32, 22)
        sqr = sb_ring(C, st, "asq", [128, 512], BF16, 3)
        yr = sb_ring(C, st, "ay", [128, 512], BF16, 2, dma=True)
        pss = PsRing([C.ps[0], C.ps[1], C.ps[6]], "ps_s")
        pso, kpso = [C.ps[2], C.ps[3]], ["pso0", "pso1"]
        psd, kpsd = [C.ps[4], C.ps[5]], ["psd0", "psd1"]
        groups = GROUPS[:8] if last else GROUPS
        Vsrc = Dm["Vt"].rearrange("(kt p) f -> p kt f", p=128)

        def load_head(hd):
            kt_, kk, sk = kTr.next()
            a.ld(kt_[:], Dm["kT"][hd * 128:(hd + 1) * 128, :], [kk], sk)
            qt_, kq = [], []
            for i in range(2):
                t_, k_, s_ = qTr[i].next()
                a.ld(t_[i * 64:(i + 1) * 64, :], Dm["qT"][hd * 128 + i * 64:hd * 128 + (i + 1) * 64, :], [k_], s_)
                qt_.append(t_)
                kq.append(k_)
            v_, kv, sv = Vr.next()
            a.ld(v_[:], Vsrc[:, :, hd * 128:(hd + 1) * 128], [kv], sv)
            return (kt_, kk, qt_, kq, v_, kv)

        nxt = load_head(0)
        deferred = []
        for hd in range(8):
            kt_, kk, qt_, kq, v_, kv = nxt
            if hd + 1 < 8:
                nxt = load_head(hd + 1)
            for (g0, W) in groups:
                kts = list(range(34)) if g0 < TL else [32, 33]
                its = [(kt, sub) for kt in kts for sub in range(2)]
                LA = 2
                pend = []
                for i in range(len(its) + LA):
                    if deferred and (i == 28 or i == len(its) + LA - 1):
                        for f in deferred:
                            f()
                        del deferred[:]
                    if i < len(its):
                        kt, sub = its[i]
                        s, ks = pss.next()
                        a.mm(s[:, :W], kt_[:, kt * 128:(kt + 1) * 128], qt_[sub][:, g0:g0 + W], True, True, [kk, kq[sub]], [ks])
                        pt, kpt, _ = pTr.next()
                        a.act(pt[:, :W], s[:, :W], AF.Exp, [ks], [kpt], scale=0.125)
                        pend.append((pt, kpt))
                    if i >= LA:
                        kt, sub = its[i - LA]
                        pt, kpt = pend[i - LA]
                        a.mm(pso[sub][:, :W], v_[:, kt, :], pt[:, :W], kt == kts[0], kt == kts[-1], [kv, kpt], [kpso[sub]])
                        a.mm(psd[sub][:, :W], C.onesb[:], pt[:, :W], kt == kts[0], kt == kts[-1], [kpt], [kpsd[sub]])
                cp = []
                for src, ksrc in ((psd[0], kpsd[0]), (pso[0], kpso[0]), (psd[1], kpsd[1]), (pso[1], kpso[1])):
                    t_, k_, _ = er.next()
                    a.copy("dve", t_[:, :W], src[:, :W], [ksrc], [k_])
                    cp.append((t_, k_))
                r1, kr1, _ = er.next()
                a.recip(r1[:, :W], cp[0][0][:, :W], [cp[0][1]], [kr1])
                o1, ko1, _ = er.next()
                a.tt("pool", o1[:, :W], cp[1][0][:, :W], r1[:, :W], ALU.mult, [cp[1][1], kr1], [ko1])
                r2, kr2, _ = er.next()
                a.recip(r2[:, :W], cp[2][0][:, :W], [cp[2][1]], [kr2])
                o2, ko2, _ = er.next()
                a.tt("pool", o2[:, :W], cp[3][0][:, :W], r2[:, :W], ALU.mult, [cp[3][1], kr2], [ko2])
                o, ko, _ = er.next()
                a.stt(o[:, :W], o2[:, :W], C.neglam[:, l:l + 1], o1[:, :W], ALU.mult, ALU.add, [ko1, ko2], [ko])

                def part_b(o=o, ko=ko, W=W, hd=hd, g0=g0):
                    sq, ksq, _ = sqr.next()
                    a.act(sq[:, :W], o[:, :W], AF.Square, [ko], [ksq])
                    pn, kpn = pss.next()
                    a.mm(pn[:, :W], C.onesb[:], sq[:, :W], True, True, [ksq], [kpn])
                    tm, ktm, _ = er.next()
                    a.act(tm[:, :W], pn[:, :W], AF.Sqrt, [kpn], [ktm], scale=1.0 / 128, bias=EPS)
                    rs, krs, _ = er.next()
                    a.recip(rs[:, :W], tm[:, :W], [ktm], [krs])
                    y, ky, sy = yr.next()
                    a.stt(y[:, :W], o[:, :W], C.attg[:, l:l + 1], rs[:, :W], ALU.mult, ALU.mult, [ko, krs], [ky])
                    a.st(Dm["yattT"][hd * 128:(hd + 1) * 128, g0:g0 + W], y[:, :W], [ky], sy)

                deferred.append(part_b)
        for f in deferred:
            f()
        sch.flush()


def gen_gmlp(C, l, st):
    nc, sch, a, Dm = C.nc, C.sch, C.a, C.D
    if True:
        s0 = sch.new_dma_sem()
        wsf = st.enter_context(nc.sbuf_tensor("gwsf", [128, 8, 128], F32))
        wsb = st.enter_context(nc.sbuf_tensor("gwsb", [128, 8, 128], BF16))
        bsbc = st.enter_context(nc.sbuf_tensor("gbsbc", [128, 8, 128], F32))
        lng = st.enter_context(nc.sbuf_tensor("glng", [128, 1024], F32))
        lnb = st.enter_context(nc.sbuf_tensor("glnb", [128, 1024], F32))
        a.ld(wsf[:], Dm["gm_wsT"][l], ["gwsf"], s0)
        a.ld(bsbc[:], Dm["gm_bsbc"][l], ["gbsbc"], sch.new_dma_sem())
        a.ld(lng[:], Dm["gm_lng"][l], ["glng"], sch.new_dma_sem())
        a.ld(lnb[:], Dm["gm_lnb"][l], ["glnb"], sch.new_dma_sem())
        a.copy("dve", wsb[:], wsf[:], ["gwsf"], ["gwsb"])
        gvr = sb_ring(C, st, "ggv", [128, 1024], F32, 2, dma=True)
        gur = sb_ring(C, st, "ggu", [128, 8, 128], F32, 2, dma=True)
        junk = st.enter_context(nc.sbuf_tensor("gjunk", [128, 1024], F32))
        str_ = sb_ring(C, st, "gst", [128, 8], F32, 2)
        t1r = sb_ring(C, st, "gt1", [128, 1024], F32, 2)
        vnr = sb_ring(C, st, "gvn", [128, 1024], BF16, 2)
        s1r = sb_ring(C, st, "gs1", [128, 8, 128], F32, 2)
        yr = sb_ring(C, st, "gy", [128, 8, 128], BF16, 2, dma=True)
        psr = PsRing([(C.ps[0], C.ps[1]), (C.ps[2], C.ps[3])], "gps")
        gusrc = Dm["guT"].rearrange("(g p) t -> p g t", p=128)
        ydst = Dm["ygmT"].rearrange("(g p) t -> p g t", p=128)

        def load(n):
            gv, kgv, sgv = gvr.next()
            a.ld(gv[:], Dm["gvg"][n * 128:(n + 1) * 128, :], [kgv], sgv)
            gu, kgu, sgu = gur.next()
            a.ld(gu[:], gusrc[:, :, n * 128:(n + 1) * 128], [kgu], sgu)
            return gv, kgv, gu, kgu

        nxt = load(0)
        for n in range(NT // 128):
            gv, kgv, gu, kgu = nxt
            if n + 1 < NT // 128:
                nxt = load(n + 1)
            s, ks, _ = str_.next()
            sch.op("dve", lambda e, s=s, gv=gv: e.reduce_sum(out=s[:, 0:1], in_=gv[:], axis=AX.X), [kgv], [ks])
            a.act(junk[:], gv[:], AF.Square, [kgv, ks], ["gjunk", ks], accum_out=s[:, 1:2])
            a.ts("dve", s[:, 2:3], s[:, 0:1], 1.0 / 1024, None, ALU.mult, ALU.bypass, [ks], [ks])
            a.tt("dve", s[:, 3:4], s[:, 2:3], s[:, 2:3], ALU.mult, [ks], [ks])
            a.stt(s[:, 4:5], s[:, 1:2], 1.0 / 1024, s[:, 3:4], ALU.mult, ALU.subtract, [ks], [ks])
            a.act(s[:, 5:6], s[:, 4:5], AF.Sqrt, [ks], [ks], scale=1.0, bias=EPS)
            a.recip(s[:, 6:7], s[:, 5:6], [ks], [ks])
            t1, kt1, _ = t1r.next()
            a.ts("dve", t1[:], gv[:], s[:, 2:3], s[:, 6:7], ALU.subtract, ALU.mult, [kgv, ks], [kt1])
            a.tt("pool", t1[:], t1[:], lng[:], ALU.mult, [kt1, "glng"], [kt1])
            vn, kvn, _ = vnr.next()
            a.tt("dve", vn[:], t1[:], lnb[:], ALU.add, [kt1, "glnb"], [kvn])
            (pa, pb), kp = psr.next()
            for g in range(8):
                pp = pa if g < 4 else pb
                a.mm(pp[:, (g % 4) * 128:(g % 4 + 1) * 128], vn[:, g * 128:(g + 1) * 128], wsb[:, g, :], True, True, [kvn, "gwsb"], [kp])
            s1, ks1, _ = s1r.next()
            a.tt("dve", s1[:, 0:4, :], pa[:].rearrange("p (g t) -> p g t", t=128), bsbc[:, 0:4, :], ALU.add, [kp, "gbsbc"], [ks1])
            a.tt("dve", s1[:, 4:8, :], pb[:].rearrange("p (g t) -> p g t", t=128), bsbc[:, 4:8, :], ALU.add, [kp, "gbsbc"], [ks1])
            y, ky, sy = yr.next()
            a.tt("pool", y[:], s1[:], gu[:], ALU.mult, [ks1, kgu], [ky])
            a.st(ydst[:, :, n * 128:(n + 1) * 128], y[:], [ky], sy)
            yield


def phase_gmlp_hprep(C, l):
    with ExitStack() as st:
        ps_alloc(C, st, with_bf16=True)
        gens = [gen_gmlp(C, l, st), gen_hgrn_prep(C, l, st)]
        while gens:
            for g in list(gens):
                try:
                    next(g)
                except StopIteration:
                    gens.remove(g)
        C.sch.flush()


SEG = 2176
NCS = SEG // HC


def gen_hgrn_prep(C, l, st):
    nc, sch, a, Dm = C.nc, C.sch, C.a, C.D
    if True:
        rmask = st.enter_context(nc.sbuf_tensor("hrmask", [128, SEG], F32))
        a.memset("pool", rmask[:], 1.0, ["hrmask"])
        a.memset("pool", rmask[:].rearrange("p (c t) -> p c t", t=HC)[:, :, 0:1], 0.0, ["hrmask"])
        lfr = sb_ring(C, st, "hlf", [128, SEG], F32, 2, dma=True)
        kgr = sb_ring(C, st, "hkg", [128, SEG], F32, 2, dma=True)
        qr = sb_ring(C, st, "hq", [128, SEG], F32, 2, dma=True)
        Ar = sb_ring(C, st, "hA", [128, SEG], F32, 2)
        Dr = sb_ring(C, st, "hD", [128, SEG], F32, 2)
        Er = sb_ring(C, st, "hE", [128, SEG], F32, 2)
        qor = sb_ring(C, st, "hqo", [128, SEG], BF16, 2, dma=True)
        kor = sb_ring(C, st, "hko", [128, SEG], BF16, 2, dma=True)
        ktr = sb_ring(C, st, "hkt", [128, 8, 128], BF16, 2, dma=True)
        scr = sb_ring(C, st, "hsc", [128, 4, NCS], F32, 2, dma=True)
        psb = PsRing([C.psb], "psb")
        units = [(hd, sg, d) for hd in range(8) for sg in range(2) for d in range(2)]
        qcur = [None]

        def load_unit(u):
            hd, sg, d = u
            t0 = sg * SEG
            if d == 0:
                q, kq, sq_ = qr.next()
                a.ld(q[:], Dm["hqT"][hd * 128:(hd + 1) * 128, t0:t0 + SEG], [kq], sq_)
                qcur[0] = (q, kq)
            lf, klf, slf = lfr.next()
            a.ld(lf[:], Dm["lfT"][d, hd * 128:(hd + 1) * 128, t0:t0 + SEG], [klf], slf)
            kg, kkg, skg = kgr.next()
            a.ld(kg[:], Dm["kgT"][d, hd * 128:(hd + 1) * 128, t0:t0 + SEG], [kkg], skg)
            return qcur[0] + (lf, klf, kg, kkg)

        nxtU = load_unit(units[0])
        for ui, (hd, sg, d) in enumerate(units):
            if True:
                t0 = sg * SEG
                if True:
                    q, kq, lf, klf, kg, kkg = nxtU
                    if ui + 1 < len(units):
                        nxtU = load_unit(units[ui + 1])
                    A_, kA, _ = Ar.next()
                    sch.op("dve", lambda e, A_=A_, lf=lf: e.tensor_tensor_scan(out=A_[:], data0=rmask[:], data1=lf[:], initial=0.0,
                                                                             op0=ALU.mult, op1=ALU.add), ["hrmask", klf], [kA])
                    A3 = A_[:].rearrange("p (c t) -> p c t", t=HC)
                    if d == 1:
                        D0, kD0, _ = Dr.next()
                        a.tt("pool", D0[:], lf[:], A_[:], ALU.subtract, [klf, kA], [kD0])
                        A2, kA2, _ = Ar.next()
                        a.tt("dve", A2[:].rearrange("p (c t) -> p c t", t=HC), D0[:].rearrange("p (c t) -> p c t", t=HC),
                             A3[:, :, HC - 1:HC].broadcast_to([128, NCS, HC]), ALU.add, [kD0, kA], [kA2])
                        A_, kA = A2, kA2
                        A3 = A_[:].rearrange("p (c t) -> p c t", t=HC)
                        iref, ilast = 16, 0
                    else:
                        iref, ilast = 15, HC - 1
                    Dd, kD, _ = Dr.next()
                    a.tt("dve", Dd[:].rearrange("p (c t) -> p c t", t=HC), A3, A3[:, :, iref:iref + 1].broadcast_to([128, NCS, HC]),
                         ALU.subtract, [kA], [kD])
                    sc, ksc, ssc = scr.next()
                    a.act(sc[:, 0, :], A3[:, :, ilast], AF.Exp, [kA], [ksc])
                    a.act(sc[:, 1, :], Dd[:].rearrange("p (c t) -> p c t", t=HC)[:, :, ilast], AF.Exp, [kD], [ksc])
                    a.act(sc[:, 2, :], A3[:, :, iref], AF.Exp, [kA], [ksc])
                    a.st(Dm["hsc"][d, hd, sg], sc[:], [ksc], ssc)
                    E1, kE1, _ = Er.next()
                    a.act(E1[:], Dd[:], AF.Exp, [kD], [kE1])
                    qo, kqo, sqo = qor.next()
                    a.tt("pool", qo[:], q[:], E1[:], ALU.mult, [kq, kE1], [kqo])
                    a.st(Dm["qtil"][d, hd * 128:(hd + 1) * 128, t0:t0 + SEG], qo[:], [kqo], sqo)
                    E2, kE2, _ = Er.next()
                    a.act(E2[:], Dd[:], AF.Exp, [kD], [kE2], scale=-1.0)
                    ko, kko, sko = kor.next()
                    a.tt("dve", ko[:], kg[:], E2[:], ALU.mult, [kkg, kE2], [kko])
                    a.st(Dm["ktilT"][d, hd * 128:(hd + 1) * 128, t0:t0 + SEG], ko[:], [kko], sko)
                    nb = SEG // 128
                    ktdst = Dm["ktok"][d, hd].rearrange("(b p) k -> p b k", p=128)
                    for b0 in range(0, nb, 8):
                        n = min(8, nb - b0)
                        pb_, kpb = psb.next()
                        for i in range(n):
                            a.tr(pb_[:, i * 128:(i + 1) * 128], ko[:, (b0 + i) * 128:(b0 + i + 1) * 128], C.identb[:], [kko, "identb"], [kpb])
                        kt_, kkt, skt = ktr.next()
                        a.copy("act" if (b0 // 8) % 2 == 0 else "dve", kt_[:, :n, :], pb_[:, :n * 128].rearrange("p (b k) -> p b k", k=128), [kpb], [kkt])
                        a.st(ktdst[:, t0 // 128 + b0:t0 // 128 + b0 + n, :], kt_[:, :n, :], [kkt], skt)
                    yield


def phase_hgrn_scan(C, l):
    nc, sch, a, Dm = C.nc, C.sch, C.a, C.D
    NB = NT // 128
    with ExitStack() as st:
        ps_alloc(C, st)
        s_ld = [sch.new_dma_sem() for _ in range(12)]
        bm = [st.enter_context(nc.sbuf_tensor("sbm%d" % d, [128, 512], F32)) for d in range(2)]
        rmk = st.enter_context(nc.sbuf_tensor("srmk", [128, 4], F32))
        for d in range(2):
            a.ld(bm[d][:], Dm["hmask"][d], [("sbm", d)], s_ld[9 + d])
        a.ld(rmk[:], Dm["hrmk"], ["srmk"], s_ld[11])
        qtr = [sb_ring(C, st, "sq%d" % d, [128, NT], BF16, 2, dma=True) for d in range(2)]
        kTr = [sb_ring(C, st, "sk%d" % d, [128, NT], BF16, 2, dma=True) for d in range(2)]
        ktkr = [sb_ring(C, st, "skt%d" % d, [128, NB, 128], BF16, 2, dma=True) for d in range(2)]
        hsr = [sb_ring(C, st, "shs%d" % d, [128, 2, 4, NCS], F32, 2, dma=True) for d in range(2)]
        vtkr = sb_ring(C, st, "svt", [128, NB, 128], BF16, 2, dma=True)
        scT = [st.enter_context(nc.sbuf_tensor("sscT%d" % d, [128, NB, 128], BF16)) for d in range(2)]
        vmk = [st.enter_context(nc.sbuf_tensor("svm%d" % j, [128, NB, 128], BF16)) for j in range(4)]
        Sr = [sb_ring(C, st, "sS%d" % d, [128, 128], F32, 3) for d in range(2)]
        Srefr = sb_ring(C, st, "sSref", [128, 128], BF16, 6)
        tmpr = sb_ring(C, st, "stmp", [128, 128], F32, 6)
        oor = sb_ring(C, st, "soo", [128, 512], F32, 4, dma=True)
        pssc = PsRing([C.ps[0], C.ps[1]], "psbank01")
        pssc.keys = [("psbank", 0), ("psbank", 1)]
        ps_ub = [[(C.ps[0], ("psbank", 0)), (C.ps[6], ("psbank", 6))], [(C.ps[1], ("psbank", 1)), (C.ps[7], ("psbank", 7))]]
        ps_o = [[C.ps[2], C.ps[3]], [C.ps[4], C.ps[5]]]
        hvsrc = Dm["hv"].rearrange("(b p) f -> p b f", p=128)
        ucnt = [0, 0]

        def load_head(hd):
            H = dict(qt=[], kT=[], ktk=[], hs=[], kq=[], kk=[], kkt=[], khs=[])
            for d in range(2):
                t_, k_, s_ = qtr[d].next()
                a.ld(t_[:], Dm["qtil"][d, hd * 128:(hd + 1) * 128, :], [k_], s_)
                H["qt"].append(t_); H["kq"].append(k_)
                t_, k_, s_ = kTr[d].next()
                a.ld(t_[:], Dm["ktilT"][d, hd * 128:(hd + 1) * 128, :], [k_], s_)
                H["kT"].append(t_); H["kk"].append(k_)
                t_, k_, s_ = ktkr[d].next()
                a.ld(t_[:], Dm["ktok"][d, hd].rearrange("(b p) k -> p b k", p=128), [k_], s_)
                H["ktk"].append(t_); H["kkt"].append(k_)
                t_, k_, s_ = hsr[d].next()
                a.ld(t_[:], Dm["hsc"][d, hd].rearrange("s p a c -> p s a c"), [k_], s_)
                H["hs"].append(t_); H["khs"].append(k_)
            t_, k_, s_ = vtkr.next()
            a.ld(t_[:], hvsrc[:, :, hd * 128:(hd + 1) * 128], [k_], s_)
            H["vtk"], H["kv"] = t_, k_
            return H

        nxtH = load_head(0)
        for hd in range(8):
            H = nxtH
            qt, kT, ktk, hs, vtk = H["qt"], H["kT"], H["ktk"], H["hs"], H["vtk"]
            for j in range(4):
                if j % 2 == 0:
                    a.ts("dve", vmk[j][:], vtk[:], rmk[:, j:j + 1], None, ALU.mult, ALU.bypass, [H["kv"], "srmk"], [("svm", j)])
                else:
                    a.act(vmk[j][:], vtk[:], AF.Copy, [H["kv"], "srmk"], [("svm", j)], scale=rmk[:, j:j + 1])
            S, kS = [None, None], [None, None]
            for d in range(2):
                S[d], kS[d], _ = Sr[d].next()
                a.memset("pool", S[d][:], 0.0, [kS[d]])
            for d in range(2):
                for b0 in range(0, NB, 4):
                    n = min(4, NB - b0)
                    p, kp = pssc.next()
                    for i in range(n):
                        b = b0 + i
                        a.mm(p[:, i * 128:(i + 1) * 128], kT[d][:, b * 128:(b + 1) * 128], qt[d][:, b * 128:(b + 1) * 128], True, True,
                             [H["kk"][d], H["kq"][d]], [kp])
                    a.tt("dve", scT[d][:, b0:b0 + n, :], p[:, :n * 128].rearrange("p (b t) -> p b t", t=128),
                         bm[d][:, :n * 128].rearrange("p (b t) -> p b t", t=128), ALU.mult, [kp, ("sbm", d)], [("sscT", d, b0)])
            if hd + 1 < 8:
                nxtH = load_head(hd + 1)
            _dbg = 9
            orders = [list(range(128, 136)) + list(range(0, 128)), list(range(135, 127, -1)) + list(range(127, -1, -1))]
            ogrp = [{}, {}]
            pend_u = [None, None]

            _skip = []

            def emit_u(d, c):
                if "u" in _skip:
                    pend_u[d] = (S[d], kS[d])
                    return
                b, j = c // 4, c % 4
                pbank, kpu = ps_ub[d][ucnt[d] % 2]
                pu = pbank[:, 0:128]
                ucnt[d] += 1
                _v = ""
                if _v == "vtk":
                    a.mm(pu, ktk[d][:, b, :], vtk[:, b, :], True, True, [H["kkt"][d], H["kv"]], [kpu])
                elif _v == "kT":
                    a.mm(pu, kT[d][:, b * 128:(b + 1) * 128], vmk[j][:, b, :], True, True, [H["kk"][d], ("svm", j)], [kpu])
                else:
                    a.mm(pu, ktk[d][:, b, :], vmk[j][:, b, :], True, True, [H["kkt"][d], ("svm", j)], [kpu])
                sgi, ci = c // NCS, c % NCS
                tmp, ktmp, _ = tmpr.next()
                if _dbg == 2:
                    a.act(tmp[:], pu, AF.Copy, [kpu, H["khs"][d]], [ktmp], scale=hs[d][:, sgi, 1, ci:ci + 1])
                else:
                    a.ts("dve", tmp[:], pu, hs[d][:, sgi, 1, ci:ci + 1], None, ALU.mult, ALU.bypass, [kpu, H["khs"][d]], [ktmp])
                pend_u[d] = (tmp, ktmp)

            for d in range(2):
                emit_u(d, orders[d][0])
            for step in range(NCH):
                srefs = []
                for d in range(2):
                    c = orders[d][step]
                    sgi, ci = c // NCS, c % NCS
                    Sref, kSref, _ = Srefr.next()
                    if "sref" not in _skip:
                        a.act(Sref[:], S[d][:], AF.Copy, [kS[d], H["khs"][d]], [kSref], scale=hs[d][:, sgi, 2, ci:ci + 1])
                    srefs.append((Sref, kSref))
                cur_u = list(pend_u)
                for d in range(2):
                    c = orders[d][step]
                    cs = slice(c * HC, (c + 1) * HC)
                    gi, so = c // 16, (c % 16) * HC
                    nin = 8 if gi == 8 else 16
                    if _dbg <= 2:
                        continue
                    if gi not in ogrp[d]:
                        bank = ps_o[d][len(ogrp[d]) % 2]
                        kbank = ("pso", d, len(ogrp[d]) % 2)
                        ogrp[d][gi] = [bank, kbank, 0]
                        blks = list(range(4 * gi, min(4 * gi + 4, NB)))
                        for i, b in enumerate(blks):
                            a.mm(bank[:, i * 128:(i + 1) * 128], vtk[:, b, :], scT[d][:, b, :], i == 0, False,
                                 [H["kv"], ("sscT", d, 4 * gi)], [kbank])
                    bank, kbank, _ = ogrp[d][gi]
                    ogrp[d][gi][2] += 1
                    Sref, kSref = srefs[d]
                    if _dbg >= 4:
                        a.mm(bank[:, so:so + HC], Sref[:], qt[d][:, cs], False, ogrp[d][gi][2] == nin, [kSref, H["kq"][d]], [kbank])
                    if ogrp[d][gi][2] == nin:
                        Wg = nin * HC
                        oo, koo, soo = oor.next()
                        a.copy("act" if d == 0 else "dve", oo[:, :Wg], bank[:, :Wg], [kbank], [koo])
                        a.st(Dm["oT"][d, hd * 128:(hd + 1) * 128, gi * 512:gi * 512 + Wg], oo[:, :Wg], [koo], soo)
                for d in range(2):
                    c = orders[d][step]
                    sgi, ci = c // NCS, c % NCS
                    tmp, ktmp = cur_u[d]
                    if "stt" in _skip:
                        continue
                    Sn, kSn, _ = Sr[d].next()
                    a.stt(Sn[:], S[d][:], hs[d][:, sgi, 0, ci:ci + 1], tmp[:], ALU.mult, ALU.add, [kS[d], ktmp, H["khs"][d]], [kSn])
                    S[d], kS[d] = Sn, kSn
                if step + 1 < NCH:
                    for d in range(2):
                        emit_u(d, orders[d][step + 1])
        sch.flush()


def phase_hgrn_out(C, l):
    nc, sch, a, Dm = C.nc, C.sch, C.a, C.D
    with ExitStack() as st:
        ps_alloc(C, st)
        ofr = sb_ring(C, st, "of", [128, 512], F32, 2, dma=True)
        obr = sb_ring(C, st, "ob", [128, 512], F32, 2, dma=True)
        hgr = sb_ring(C, st, "ohg", [128, 512], F32, 2, dma=True)
        er = sb_ring(C, st, "oe", [128, 512], F32, 6)
        sqr = sb_ring(C, st, "osq", [128, 512], BF16, 2)
        yr = sb_ring(C, st, "oy", [128, 512], BF16, 2, dma=True)
        pss = PsRing([C.ps[0], C.ps[1]], "opn")
        gcol = C.vec[:, l * V_PER + V_HGG:l * V_PER + V_HGG + 1]
        items = [(hd, g0, W) for hd in range(8) for (g0, W) in GROUPS]

        def load(it):
            hd, g0, W = it
            rows = slice(hd * 128, (hd + 1) * 128)
            of, kof, sof = ofr.next()
            a.ld(of[:, :W], Dm["oT"][0, rows, g0:g0 + W], [kof], sof)
            ob, kob, sob = obr.next()
            a.ld(ob[:, :W], Dm["oT"][1, rows, g0:g0 + W], [kob], sob)
            hg, khg, shg = hgr.next()
            a.ld(hg[:, :W], Dm["hgT"][rows, g0:g0 + W], [khg], shg)
            return of, kof, ob, kob, hg, khg

        nxt = load(items[0])
        for i, (hd, g0, W) in enumerate(items):
            of, kof, ob, kob, hg, khg = nxt
            if i + 1 < len(items):
                nxt = load(items[i + 1])
            o, ko, _ = er.next()
            a.tt("pool", o[:, :W], of[:, :W], ob[:, :W], ALU.add, [kof, kob], [ko])
            sq, ksq, _ = sqr.next()
            a.act(sq[:, :W], o[:, :W], AF.Square, [ko], [ksq])
            pn, kpn = pss.next()
            a.mm(pn[:, :W], C.onesb[:], sq[:, :W], True, True, [ksq], [kpn])
            tm, ktm, _ = er.next()
            a.act(tm[:, :W], pn[:, :W], AF.Sqrt, [kpn], [ktm], scale=1.0 / 128, bias=EPS)
            rs, krs, _ = er.next()
            a.recip(rs[:, :W], tm[:, :W], [ktm], [krs])
            y1, ky1, _ = er.next()
            a.stt(y1[:, :W], o[:, :W], gcol, rs[:, :W], ALU.mult, ALU.mult, [ko, krs], [ky1])
            y, ky, sy = yr.next()
            a.tt("pool", y[:, :W], y1[:, :W], hg[:, :W], ALU.mult, [ky1, khg], [ky])
            a.st(Dm["yhgT"][hd * 128:(hd + 1) * 128, g0:g0 + W], y[:, :W], [ky], sy)
        sch.flush()


def load_weight_bf16(C, st, name, src3, KC, NF, stage_ring):
    a = C.a
    wb = st.enter_context(C.nc.sbuf_tensor(name, [128, KC, NF], BF16))
    i = 0
    for kc0 in range(0, KC, 8):
        kn = min(8, KC - kc0)
        for f0 in range(0, NF, 512):
            wt, kw, sw = stage_ring.next()
            a.ld(wt[:, :kn, :], src3[:, kc0:kc0 + kn, f0:f0 + 512], [kw], sw)
            a.copy("pool" if i % 2 == 0 else "dve", wb[:, kc0:kc0 + kn, f0:f0 + 512], wt[:, :kn, :], [kw], [(name, kc0, f0)])
            i += 1
    return wb


def phase_merge(C, l):
    nc, sch, a, Dm = C.nc, C.sch, C.a, C.D
    with ExitStack() as st:
        ps_alloc(C, st)
        stage = sb_ring(C, st, "mstage", [128, 8, 512], F32, 2, dma=True)
        wn = ["w_br_att", "w_br_gm", "w_br_hg"]
        wbr = [load_weight_bf16(C, st, "m" + n, Dm[n][l].rearrange("(kc p) f -> p kc f", p=128), 8, 1024, stage) for n in wn]
        wkeys = [[("m" + n, 0, f0) for f0 in (0, 512)] for n in wn]
        ysrc = [fmview(Dm[n]) for n in ("yattT", "ygmT", "yhgT")]
        yr = [sb_ring(C, st, "my%d" % i, [128, 8, 512], BF16, 2, dma=True) for i in range(3)]
        ymr = sb_ring(C, st, "mym", [128, 8, 512], BF16, 2, dma=True)
        gr = sb_ring(C, st, "mg", [128, 512], F32, 6, dma=True)
        accr = sb_ring(C, st, "macc", [128, 512], F32, 4)
        psr = PsRing(C.ps[0:6], "mps")
        ymdst = fmview(Dm["ymT"])

        def load(g0, W):
            out = []
            for i in range(3):
                y, ky, sy = yr[i].next()
                a.ld(y[:, :, :W], ysrc[i][:, :, g0:g0 + W], [ky], sy)
                out.append((y, ky))
            return out

        nxt = load(*GROUPS[0])
        for gi, (g0, W) in enumerate(GROUPS):
            ys = nxt
            if gi + 1 < len(GROUPS):
                nxt = load(*GROUPS[gi + 1])
            ym, kym, sym = ymr.next()
            for fo in range(8):
                gts = []
                for br in range(3):
                    g, kg, sg = gr.next()
                    a.ld(g[:, :W], Dm["gatesT"][(br * 8 + fo) * 128:(br * 8 + fo + 1) * 128, g0:g0 + W], [kg], sg)
                    gts.append((g, kg))
                acc, kacc = None, None
                for br in range(3):
                    p, kp = psr.next()
                    for kc in range(8):
                        a.mm(p[:, :W], wbr[br][:, kc, fo * 128:(fo + 1) * 128], ys[br][0][:, kc, :W], kc == 0, kc == 7,
                             [ys[br][1]] + wkeys[br], [kp])
                    g, kg = gts[br]
                    t, kt, _ = accr.next()
                    a.tt("dve", t[:, :W], p[:, :W], g[:, :W], ALU.mult, [kp, kg], [kt])
                    if br == 0:
                        acc, kacc = t, kt
                    elif br == 1:
                        t2, kt2, _ = accr.next()
                        a.tt("pool", t2[:, :W], acc[:, :W], t[:, :W], ALU.add, [kacc, kt], [kt2])
                        acc, kacc = t2, kt2
                    else:
                        a.tt("pool", ym[:, fo, :W], acc[:, :W], t[:, :W], ALU.add, [kacc, kt], [kym])
            a.st(ymdst[:, :, g0:g0 + W], ym[:, :, :W], [kym], sym)
        sch.flush()


def phase_outproj(C, l):
    nc, sch, a, Dm = C.nc, C.sch, C.a, C.D
    with ExitStack() as st:
        ps_alloc(C, st)
        stage = sb_ring(C, st, "ostage", [128, 8, 512], F32, 2, dma=True)
        wo = load_weight_bf16(C, st, "owout", Dm["w_out"][l].rearrange("(kc p) f -> p kc f", p=128), 8, 1024, stage)
        wk = [("owout", 0, 0), ("owout", 0, 512)]
        R = norm_rings(C, st)
        R["psn"] = PsRing([C.ps[6]], "psn")
        ymr = sb_ring(C, st, "oym", [128, 8, 512], BF16, 2, dma=True)
        xr = sb_ring(C, st, "ox", [128, 8, 512], F32, 2, dma=True)
        zr = sb_ring(C, st, "oz", [128, 8, 512], F32, 2)
        h2r = sb_ring(C, st, "oh2", [128, 8, 512], BF16, 2, dma=True)
        psr = PsRing(C.ps[0:6], "ops")
        ymsrc, xv, h2dst = fmview(Dm["ymT"]), fmview(Dm["xT"]), fmview(Dm["h2T"])
        gg = C.modv[:, l, 2]

        def load(g0, W):
            ym, kym, sym = ymr.next()
            a.ld(ym[:, :, :W], ymsrc[:, :, g0:g0 + W], [kym], sym)
            x, kx, sx = xr.next()
            a.ld(x[:, :, :W], xv[:, :, g0:g0 + W], [kx], sx)
            return ym, kym, x, kx, sx

        nxt = load(*GROUPS[0])
        for gi, (g0, W) in enumerate(GROUPS):
            ym, kym, x, kx, sx = nxt
            if gi + 1 < len(GROUPS):
                nxt = load(*GROUPS[gi + 1])
            j = 1 if g0 >= TL else 0
            z, kz, _ = zr.next()
            for fo in range(8):
                p, kp = psr.next()
                for kc in range(8):
                    a.mm(p[:, :W], wo[:, kc, fo * 128:(fo + 1) * 128], ym[:, kc, :W], kc == 0, kc == 7, [kym] + wk, [kp])
                a.copy("act" if fo % 2 == 0 else "dve", z[:, fo, :W], p[:, :W], [kp], [kz])
            sq, ksq, _ = R["sq"].next()
            psn, kpsn = R["psn"].next()
            tmp, ktmp, _ = R["ntmp"].next()
            rstd, krstd, _ = R["rstd"].next()
            fm_rstd(C, z[:, :, :W], kz, 8, W, DM, psn, kpsn, sq, ksq, tmp, ktmp, rstd, krstd)
            for kc in range(8):
                tf, ktf, _ = R["tmpf"].next()
                a.stt(tf[:, :W], z[:, kc, :W], gg[:, kc, j:j + 1], rstd[:, :W], ALU.mult, ALU.mult, [kz, krstd], [ktf])
                a.tt("pool", x[:, kc, :W], x[:, kc, :W], tf[:, :W], ALU.add, [kx, ktf], [kx])
            a.st(xv[:, :, g0:g0 + W], x[:, :, :W], [kx], sx)
            h2, kh2, sh2 = h2r.next()
            norm_mod(C, R, x[:, :, :W], kx, W, g0, l, 1, dst=lambda kc, h2=h2, kh2=kh2, W=W: (h2[:, kc, :W], kh2))
            a.st(h2dst[:, :, g0:g0 + W], h2[:, :, :W], [kh2], sh2)
        sch.flush()


def acol(t):
    return t + 1 if t < TL else t + 3


def phase_ffn_up(C, l):
    nc, sch, a, Dm = C.nc, C.sch, C.a, C.D
    with ExitStack() as st:
        ps_alloc(C, st)
        s0 = sch.new_dma_sem()
        a.ld(C.hT[:], fmview(Dm["h2T"]), ["hTall"], s0)
        wsr = sb_ring(C, st, "fws", [128, 8, 128], F32, 4, dma=True)
        wbr = sb_ring(C, st, "fwb", [128, 8, 128], BF16, 4)
        accr = sb_ring(C, st, "facc", [128, NT + 4], F32, 4)
        ulr = sb_ring(C, st, "ful", [128, 16], F32, 4)
        mr = sb_ring(C, st, "fm", [128, NT + 4], BF16, 2, dma=True)
        psr = PsRing(C.ps[0:6], "fps")
        wsrc = Dm["w_up"][l].rearrange("(kc p) f -> p kc f", p=128)
        vb = l * V_PER

        def load_w(fc):
            wt, kw, sw = wsr.next()
            a.ld(wt[:], wsrc[:, :, fc * 128:(fc + 1) * 128], [kw], sw)
            wb, kb, _ = wbr.next()
            a.copy("pool", wb[:], wt[:], [kw], [kb])
            return wb, kb

        def conv_chunk(fc, wb, kb):
            acc, kacc0, _ = accr.next()
            ul, kul, _ = ulr.next()
            cw = [C.vec[:, vb + V_CW + j * 44 + fc:vb + V_CW + j * 44 + fc + 1] for j in range(3)]
            cb = C.vec[:, vb + V_CB + fc:vb + V_CB + fc + 1]
            kg = [(kacc0, gi) for gi in range(len(GROUPS))]
            for gi, (g0, W) in enumerate(GROUPS):
                p, kp = psr.next()
                for kc in range(8):
                    a.mm(p[:, :W], wb[:, kc, :], C.hT[:, kc, g0:g0 + W], kc == 0, kc == 7, [kb, "hTall"], [kp])
                c0 = acol(g0)
                prev = [kg[gi - 1]] if gi >= 1 else []
                a.act(acc[:, c0:c0 + W], p[:, :W], AF.Identity, [kp], [kg[gi]], scale=cw[1], bias=cb)
                a.copy("act", ul[:, gi:gi + 1], p[:, W - 1:W], [kp], [(kul, gi)])
                a.stt(acc[:, c0 + 1:c0 + W], p[:, 0:W - 1], cw[0], acc[:, c0 + 1:c0 + W], ALU.mult, ALU.add, [kp, kg[gi]], [kg[gi]])
                if gi >= 1 and g0 != TL:
                    a.stt(acc[:, c0:c0 + 1], ul[:, gi - 1:gi], cw[0], acc[:, c0:c0 + 1], ALU.mult, ALU.add, [(kul, gi - 1), kg[gi]], [kg[gi]])
                a.stt(acc[:, c0 - 1:c0 + W - 1], p[:, 0:W], cw[2], acc[:, c0 - 1:c0 + W - 1], ALU.mult, ALU.add, [kp, kg[gi]] + prev, [kg[gi]] + prev)
            kacc = kg
            return acc, kacc

        nxt = (load_w(0), load_w(NFF))
        for fc in range(NFF):
            (wa, ka), (wb_, kb_) = nxt
            if fc + 1 < NFF:
                nxt = (load_w(fc + 1), load_w(NFF + fc + 1))
            aa, kaa = conv_chunk(fc, wa, ka)
            ab, kab = conv_chunk(NFF + fc, wb_, kb_)
            a.act(aa[:], aa[:], AF.Silu, kaa, kaa)
            m, km, sm = mr.next()
            a.tt("pool", m[:], aa[:], ab[:], ALU.mult, kaa + kab, [km])
            a.st(Dm["mT"][fc * 128:(fc + 1) * 128, 0:TL], m[:, 1:TL + 1], [km], sm)
            a.st(Dm["mT"][fc * 128:(fc + 1) * 128, TL:NT], m[:, TL + 3:NT + 3], [km], sm)
        sch.flush()


GROUPS256 = [(i * 256, 256) for i in range(NT // 256)]


def phase_ffn_down(C, l):
    nc, sch, a, Dm = C.nc, C.sch, C.a, C.D
    last = (l == 1)
    W = 256
    with ExitStack() as st:
        ps_alloc(C, st)
        stage = sb_ring(C, st, "dstage", [128, 2, 512], F32, 2, dma=True)
        wd = st.enter_context(nc.sbuf_tensor("dwd", [128, NFF, 1024], BF16))
        wsrc = Dm["w_down"][l].rearrange("(kc p) f -> p kc f", p=128)
        i = 0
        for kc0 in range(0, NFF, 2):
            for f0 in (0, 512):
                wt, kw, sw = stage.next()
                a.ld(wt[:], wsrc[:, kc0:kc0 + 2, f0:f0 + 512], [kw], sw)
                a.copy("pool" if i % 2 == 0 else "dve", wd[:, kc0:kc0 + 2, f0:f0 + 512], wt[:], [kw], ["dwd"])
                i += 1
        R = norm_rings(C, st, W)
        R["psn"] = PsRing([C.ps[6]], "psn")
        mr = sb_ring(C, st, "dm", [128, NFF, W], BF16, 2, dma=True)
        xr = sb_ring(C, st, "dx", [128, 8, W], F32, 2, dma=True)
        zr = sb_ring(C, st, "dz", [128, 8, W], F32, 2)
        psr = PsRing(C.ps[0:4], "dps")
        pst = PsRing([(C.ps[4], C.ps[5])], "dpst")
        otr = sb_ring(C, st, "dot", [128, 1024], F32, 2, dma=True)
        msrc, xv = fmview(Dm["mT"]), fmview(Dm["xT"])
        gg = C.modv[:, l, 5]

        def load(g0):
            m, km, sm = mr.next()
            a.ld(m[:], msrc[:, :, g0:g0 + W], [km], sm)
            x, kx, sx = xr.next()
            a.ld(x[:], xv[:, :, g0:g0 + W], [kx], sx)
            return m, km, x, kx, sx

        groups = GROUPS256[:TL // 256] if last else GROUPS256
        nxt = load(groups[0][0])
        for gi, (g0, _) in enumerate(groups):
            m, km, x, kx, sx = nxt
            if gi + 1 < len(groups):
                nxt = load(groups[gi + 1][0])
            j = 1 if g0 >= TL else 0
            z, kz, _ = zr.next()
            for fo in range(8):
                p, kp = psr.next()
                for kc in range(NFF):
                    a.mm(p[:, :W], wd[:, kc, fo * 128:(fo + 1) * 128], m[:, kc, :], kc == 0, kc == NFF - 1, [km, "dwd"], [kp])
                a.copy("act" if fo % 2 == 0 else "dve", z[:, fo, :], p[:, :W], [kp], [kz])
            sq, ksq, _ = R["sq"].next()
            psn, kpsn = R["psn"].next()
            tmp, ktmp, _ = R["ntmp"].next()
            rstd, krstd, _ = R["rstd"].next()
            fm_rstd(C, z[:], kz, 8, W, DM, psn, kpsn, sq, ksq, tmp, ktmp, rstd, krstd)
            for kc in range(8):
                tf, ktf, _ = R["tmpf"].next()
                a.stt(tf[:, :W], z[:, kc, :], gg[:, kc, j:j + 1], rstd[:, :W], ALU.mult, ALU.mult, [kz, krstd], [ktf])
                a.tt("pool", x[:, kc, :], x[:, kc, :], tf[:, :W], ALU.add, [kx, ktf], [kx])
            if not last:
                a.st(xv[:, :, g0:g0 + W], x[:], [kx], sx)
                norm_mod(C, R, x[:], kx, W, g0, l + 1, 0)
            else:
                for ti in range(W // 128):
                    (pa, pb), kp = pst.next()
                    for kc in range(8):
                        pp = pa if kc < 4 else pb
                        a.tr(pp[:, (kc % 4) * 128:(kc % 4 + 1) * 128], x[:, kc, ti * 128:(ti + 1) * 128], C.ident[:], [kx], [kp])
                    ot, kot, sot = otr.next()
                    a.copy("dve", ot[:, 0:512], pa[:], [kp], [kot])
                    a.copy("act", ot[:, 512:1024], pb[:], [kp], [kot])
                    a.st(Dm["y"][g0 + ti * 128:g0 + (ti + 1) * 128, :], ot[:], [kot], sot)
        sch.flush()


IN_SHAPES = {
    "xin": ([NT, DM], F32), "cc": ([128, 8, 2], F32), "vec": ([128, 2 * V_PER], F32), "lamv": ([128, 2, 4, 64], F32),
    "w_ada": ([2, DM, 6 * DM], F32), "w_in": ([2, DM, W_EXT], F32), "ropeC": ([128, NT], F32), "ropeS": ([128, NT], F32),
    "gm_wsT": ([2, 128, 8, 128], F32), "gm_bsbc": ([2, 128, 8, 128], F32), "gm_lng": ([2, 128, 1024], F32), "gm_lnb": ([2, 128, 1024], F32),
    "w_br_att": ([2, DM, DM], F32), "w_br_gm": ([2, DM, DM], F32), "w_br_hg": ([2, DM, DM], F32), "w_out": ([2, DM, DM], F32),
    "w_up": ([2, DM, 2 * DFF], F32), "w_down": ([2, DFF, DM], F32),
    "hmask": ([2, 128, 512], F32), "hrmk": ([128, 4], F32),
}
SCRATCH = {
    "xT": ([DM, NT], F32), "qT": ([DM, NT], BF16), "kT": ([DM, NT], BF16), "Vt": ([NT, DM], BF16), "guT": ([DM, NT], F32),
    "gvg": ([NT, DM], F32), "hqT": ([DM, NT], F32), "lfT": ([2, DM, NT], F32), "kgT": ([2, DM, NT], F32), "hv": ([NT, DM], BF16),
    "hgT": ([DM, NT], F32), "gatesT": ([3 * DM, NT], F32), "yattT": ([DM, NT], BF16), "ygmT": ([DM, NT], BF16), "yhgT": ([DM, NT], BF16),
    "qtil": ([2, DM, NT], BF16), "ktilT": ([2, DM, NT], BF16), "ktok": ([2, 8, NT, 128], BF16), "hsc": ([2, 8, 2, 128, 4, NCS], F32),
    "oT": ([2, DM, NT], F32), "ymT": ([DM, NT], BF16), "h2T": ([DM, NT], BF16), "mT": ([DFF, NT], BF16),
}
PHASES = ["prologue", "inproj0", "attn0", "gmlp0", "hprep0", "hscan0", "hout0", "merge0", "outproj0", "ffnup0", "ffndown0",
          "inproj1", "attn1", "gmlp1", "hprep1", "hscan1", "hout1", "merge1", "outproj1", "ffnup1", "ffndown1"]


def build_nc(stop_after=None, dbg=()):
    nc = bass.Bass("TRN2", target_bir_lowering=False)
    Dm = {}
    for n, (shp, dt) in IN_SHAPES.items():
        Dm[n] = nc.dram_tensor("i_" + n, shp, dt, kind="ExternalInput").ap()
    for n, (shp, dt) in SCRATCH.items():
        Dm[n] = nc.dram_tensor("s_" + n, shp, dt, kind="ExternalOutput" if n in dbg else "Internal").ap()
    Dm["y"] = nc.dram_tensor("y", [TL, DM], F32, kind="ExternalOutput").ap()
    with ExitStack() as st0:
        C = Ctx()
        C.nc, C.D = NCProxy(nc), Dm
        C.sch = Sched(nc, st0, n_dma_sems=48)
        C.a = A(C.sch)
        sb = lambda n, shp, dt=F32: st0.enter_context(C.nc.sbuf_tensor(n, shp, dt))
        C.ident, C.identb, C.onesb = sb("ident", [128, 128]), sb("identb", [128, 128], BF16), sb("onesb", [128, 128], BF16)
        C.vec, C.modv = sb("vec", [128, 2 * V_PER]), sb("modv", [128, 2, 6, 8, 2])
        C.lb, C.oml = sb("lb", [128, 2, 2, 8]), sb("oml", [128, 2, 2, 8])
        C.neglam, C.attg = sb("neglam", [128, 2]), sb("attg", [128, 2])
        C.maskf, C.maskb = sb("maskf", [64, 32]), sb("maskb", [64, 32])
        plan = [
            ("hT+",), ("prologue", phase_prologue), ("inproj0", phase_inproj, 0), ("hT-",),
            ("attn0", phase_attn, 0), ("hprep0", phase_gmlp_hprep, 0), ("hscan0", phase_hgrn_scan, 0),
            ("hout0", phase_hgrn_out, 0), ("merge0", phase_merge, 0), ("outproj0", phase_outproj, 0),
            ("hT+",), ("ffnup0", phase_ffn_up, 0), ("ffndown0", phase_ffn_down, 0), ("inproj1", phase_inproj, 1), ("hT-",),
            ("attn1", phase_attn, 1), ("hprep1", phase_gmlp_hprep, 1), ("hscan1", phase_hgrn_scan, 1),
            ("hout1", phase_hgrn_out, 1), ("merge1", phase_merge, 1), ("outproj1", phase_outproj, 1),
            ("hT+",), ("ffnup1", phase_ffn_up, 1), ("ffndown1", phase_ffn_down, 1), ("hT-",),
        ]
        hst = None
        for item in plan:
            if item[0] == "hT+":
                hst = ExitStack()
                C.hT = hst.enter_context(C.nc.sbuf_tensor("hT", [128, 8, NT], BF16))
                continue
            if item[0] == "hT-":
                hst.close()
                hst = None
                continue
            item[1](C, *item[2:])
            if stop_after == item[0]:
                break
        if hst is not None:
            hst.close()
    return nc


def _col(v):
    return np.ascontiguousarray(v.reshape(-1, 128).T)


def prep_shared(inp):
    f32 = np.float32
    sh = {}
    sh["w_ada"] = np.ascontiguousarray(inp["w_ada"], dtype=f32)
    w_in = inp["w_in"]
    offs = np.cumsum([0, 1024, 1024, 1024, 1024, 1024, 1024, 1024, 1024, 1024, 1024, 3072])
    aq, ak, av, gu, gv, hq, hff, hfb, hi, hg, gates = [w_in[:, :, offs[i]:offs[i + 1]] for i in range(11)]
    perm64 = np.concatenate([np.arange(16, 32), np.arange(0, 16), np.arange(48, 64), np.arange(32, 48)])
    perm128 = np.concatenate([perm64, 64 + perm64])
    cols = []
    for src in (aq, ak):
        for hd in range(8):
            blk = src[:, :, hd * 128:(hd + 1) * 128]
            cols += [blk, blk[:, :, perm128]]
    cols += [gu, hq, hff, hfb, hg, gates, av, gv, hi]
    sh["w_in"] = np.ascontiguousarray(np.concatenate(cols, axis=2), dtype=f32)
    assert sh["w_in"].shape[2] == W_EXT
    half = 16
    inv = (10000.0 ** (-np.arange(half, dtype=np.float32) / half)).astype(f32)
    t = np.arange(TL)
    rows, colsp = (t // 64).astype(f32), (t % 64).astype(f32)
    Cq = np.ones((64, NT), f32)
    Sq = np.zeros((64, NT), f32)
    for base, pos in ((0, rows), (32, colsp)):
        ang = pos[None, :] * inv[:, None]
        c, s_ = np.cos(ang).astype(f32), np.sin(ang).astype(f32)
        Cq[base:base + 16, :TL] = c
        Cq[base + 16:base + 32, :TL] = c
        Sq[base:base + 16, :TL] = -s_
        Sq[base + 16:base + 32, :TL] = s_
    sh["ropeC"] = np.ascontiguousarray(np.concatenate([Cq, Cq], axis=0))
    sh["ropeS"] = np.ascontiguousarray(np.concatenate([Sq, Sq], axis=0))
    si, ti = np.arange(128)[:, None], np.arange(128)[None, :]
    same = (si // HC) == (ti // HC)
    mf = (same & (si <= ti)).astype(f32)
    mb = (same & (si >= ti)).astype(f32)
    sh["hmask"] = np.ascontiguousarray(np.stack([np.tile(mf, (1, 4)), np.tile(mb, (1, 4))]))
    sh["hrmk"] = np.ascontiguousarray((np.arange(128)[:, None] // HC == np.arange(4)[None, :]).astype(f32))
    vec = np.zeros((128, 2 * V_PER), f32)
    for l in range(2):
        b = l * V_PER
        vec[:, b + V_BADA:b + V_BADA + 48] = _col(inp["b_ada"][l])
        vec[:, b + V_GPRE1:b + V_GPRE1 + 8] = _col(inp["g_pre_mix"][l])
        vec[:, b + V_GPOST1:b + V_GPOST1 + 8] = _col(inp["g_post_mix"][l])
        vec[:, b + V_GPRE2:b + V_GPRE2 + 8] = _col(inp["g_pre_ffn"][l])
        vec[:, b + V_GPOST2:b + V_GPOST2 + 8] = _col(inp["g_post_ffn"][l])
        for j in range(3):
            vec[:, b + V_CW + j * 44:b + V_CW + (j + 1) * 44] = _col(inp["conv_w"][l, j])
        vec[:, b + V_CB:b + V_CB + 44] = _col(inp["conv_b"][l])
        vec[:, b + V_ATTG] = inp["att_subln_g"][l]
        vec[:, b + V_HGG] = inp["hg_norm_g"][l]
    for l2 in range(2):
        for d in range(2):
            vec[:, V_LB + l2 * 16 + d * 8:V_LB + l2 * 16 + d * 8 + 8] = _col(inp["hg_lb"][l2, d])
    sh["vec"] = vec
    lamv = np.stack([np.stack([inp[n][l] for n in ("lam_q1", "lam_k1", "lam_q2", "lam_k2")]) for l in range(2)])
    sh["lamv"] = np.ascontiguousarray(np.broadcast_to(lamv[None], (128, 2, 4, 64)), dtype=f32)
    sh["gm_wsT"] = np.ascontiguousarray(np.transpose(inp["gm_ws"], (0, 3, 1, 2)), dtype=f32)
    sh["gm_bsbc"] = np.ascontiguousarray(np.broadcast_to(inp["gm_bs"][:, None], (2, 128, 8, 128)), dtype=f32)
    sh["gm_lng"] = np.ascontiguousarray(np.broadcast_to(inp["gm_ln_g"][:, None], (2, 128, 1024)), dtype=f32)
    sh["gm_lnb"] = np.ascontiguousarray(np.broadcast_to(inp["gm_ln_b"][:, None], (2, 128, 1024)), dtype=f32)
    for n in ("w_br_att", "w_br_gm", "w_br_hg", "w_out", "w_up", "w_down"):
        sh[n] = np.ascontiguousarray(inp[n], dtype=f32)
    return sh


def prep_core(inp, b):
    f32 = np.float32
    d = {}
    d["xin"] = np.ascontiguousarray(np.concatenate([inp["x"][b], inp["ctx"][b]], axis=0), dtype=f32)
    cc = np.stack([_col(inp["c"][b]), _col(inp["c_ctx"])], axis=-1)
    d["cc"] = np.ascontiguousarray(cc, dtype=f32)
    return d


def kernel(**inputs):
    inp = {k: np.asarray(v) for k, v in inputs.items()}
    nc = build_nc()
    sh = prep_shared(inp)
    in_maps = []
    for b in range(8):
        m = dict(sh)
        m.update(prep_core(inp, b))
        in_maps.append({"i_" + k: v for k, v in m.items()})
    res = run_bass_kernel_spmd(nc, in_maps, core_ids=list(range(8)))
    return np.stack([np.asarray(r["y"], dtype=np.float32) for r in res.results], axis=0)
```
